# Optimizing a Trainium2 kernel written in Bass

```python
import jax
import jax.numpy as jnp
from jax import lax
import numpy as np

D_MODEL = 2048
BATCH = 8
SEQ = 4096
DEPTH = 4

GRID_W = 64
CTX_LEN = 256
EPS = 1e-6
NEG_BIG = -1e30
LB_FLOOR = 1e-30

A_WIDTH = D_MODEL // 4
A_GROUPS = 4
A_GROUP_DIM = A_WIDTH // A_GROUPS
A_CHUNK = 128

B_WIDTH = D_MODEL // 2
B_HEADS = 4
B_HEAD_DIM = B_WIDTH // B_HEADS
B_CHUNK = 128
CONV_W = 3

C_WIDTH = D_MODEL // 4
C_HEADS = 4
C_KEY_DIM = 128
C_KEY_WIDTH = C_HEADS * C_KEY_DIM
C_VAL_DIM = C_WIDTH // C_HEADS
C_CHUNK = 64

D_MIX = A_WIDTH + B_WIDTH + C_WIDTH
FORGET_BIAS_LO = 3.0
FORGET_BIAS_HI = 6.0

IN_LAYOUT = (
    ('a_u', A_WIDTH), ('a_v', A_WIDTH), ('a_z', A_WIDTH),
    ('b_q', B_WIDTH), ('b_k', B_WIDTH), ('b_v', B_WIDTH), ('b_o', B_WIDTH), ('b_z', B_WIDTH),
    ('b_i_fwd', B_HEADS), ('b_f_fwd', B_HEADS), ('b_i_bwd', B_HEADS), ('b_f_bwd', B_HEADS),
    ('c_q', C_KEY_WIDTH), ('c_f_fwd', C_KEY_WIDTH), ('c_f_bwd', C_KEY_WIDTH), ('c_i', C_WIDTH), ('c_g', C_WIDTH),
)
D_IN = sum(w for _, w in IN_LAYOUT)

kernel_name = 'hybrid_ctx_prefix_chunkmlp_mlstm_hgrn2'


def _col_start(name):
    start = 0
    for n, w in IN_LAYOUT:
        if n == name:
            return start
        start += w
    raise ValueError(name)


def _split_cols(p):
    out = {}
    start = 0
    for name, w in IN_LAYOUT:
        out[name] = p[..., start:start + w]
        start += w
    return out


def _rmsnorm(x, g):
    xf = x.astype(jnp.float32)
    y = xf * lax.rsqrt(jnp.mean(xf * xf, axis=-1, keepdims=True) + EPS)
    return (y * g.astype(jnp.float32)).astype(x.dtype)


def _head_rmsnorm(x, g, n_heads):
    b, t, w = x.shape
    xh = x.reshape(b, t, n_heads, w // n_heads).astype(jnp.float32)
    xh = xh * lax.rsqrt(jnp.mean(xh * xh, axis=-1, keepdims=True) + EPS)
    return (xh.reshape(b, t, w) * g.astype(jnp.float32)).astype(x.dtype)


def _to_heads(x, n_heads):
    b, t, w = x.shape
    return x.reshape(b, t, n_heads, w // n_heads).transpose(0, 2, 1, 3)


def _from_heads(x):
    b, h, t, d = x.shape
    return x.transpose(0, 2, 1, 3).reshape(b, t, h * d)


def _to_colmajor(x, rows):
    b, t, ch = x.shape
    return x.reshape(b, rows, GRID_W, ch).transpose(0, 2, 1, 3).reshape(b, t, ch)


def _from_colmajor(x, rows):
    b, t, ch = x.shape
    return x.reshape(b, GRID_W, rows, ch).transpose(0, 2, 1, 3).reshape(b, t, ch)


def _conv_centred(x, w):
    pad = CONV_W // 2
    t = x.shape[1]
    xp = jnp.pad(x, ((0, 0), (pad, pad), (0, 0)))
    out = w[0] * xp[:, 0:t]
    for j in range(1, CONV_W):
        out = out + w[j] * xp[:, j:j + t]
    return out


def _chunk_mlp(u, v, z, w_s, b_s):
    b, t, _ = v.shape
    n = t // A_CHUNK
    vf = v.astype(jnp.float32).reshape(b, n, A_CHUNK, A_GROUPS, A_GROUP_DIM)
    mu = jnp.mean(vf, axis=-1, keepdims=True)
    var = jnp.mean(jnp.square(vf - mu), axis=-1, keepdims=True)
    vn = ((vf - mu) * lax.rsqrt(var + EPS)).astype(v.dtype)
    mixed = jnp.einsum('gts,bnsgc->bntgc', w_s, vn) + b_s.T[:, :, None]
    return u * mixed.reshape(b, t, A_WIDTH) * jax.nn.silu(z)


def _mlstm_zero_state(b):
    return (jnp.zeros((b, B_HEADS, B_HEAD_DIM, B_HEAD_DIM), jnp.float32),
            jnp.zeros((b, B_HEADS, B_HEAD_DIM), jnp.float32),
            jnp.zeros((b, B_HEADS), jnp.float32))


def _mlstm_dir(q, k, v, ig, lf, state):
    b, h, t, d = q.shape
    L = B_CHUNK
    nc = t // L
    mask = jnp.tril(jnp.ones((L, L), bool))

    def chunks(a):
        return jnp.moveaxis(a.reshape((b, h, nc, L) + a.shape[3:]), 2, 0)

    def step(carry, inp):
        C, n, m = carry
        qc, kc, vc, ic, fc = (a.astype(jnp.float32) for a in inp)
        cb = jnp.cumsum(fc, axis=-1)
        dmat = jnp.where(mask, cb[..., :, None] - cb[..., None, :] + ic[..., None, :], NEG_BIG)
        inter = cb + m[..., None]
        mt = jnp.maximum(inter, jnp.max(dmat, axis=-1))
        scores = jnp.einsum('bhtd,bhsd->bhts', qc, kc) * jnp.exp(dmat - mt[..., None])
        e_inter = jnp.exp(inter - mt)
        num = jnp.einsum('bhts,bhse->bhte', scores, vc) + e_inter[..., None] * jnp.einsum('bhtd,bhde->bhte', qc, C)
        den = jnp.sum(scores, axis=-1) + e_inter * jnp.einsum('bhtd,bhd->bht', qc, n)
        hc = num / jnp.maximum(jnp.abs(den), jnp.exp(-mt))[..., None]
        cl = cb[..., -1]
        w_log = cl[..., None] - cb + ic
        m_new = jnp.maximum(cl + m, jnp.max(w_log, axis=-1))
        ws = jnp.exp(w_log - m_new[..., None])
        ec = jnp.exp(cl + m - m_new)
        C_new = ec[..., None, None] * C + jnp.einsum('bhs,bhsd,bhse->bhde', ws, kc, vc)
        n_new = ec[..., None] * n + jnp.einsum('bhs,bhsd->bhd', ws, kc)
        return (C_new, n_new, m_new), hc.astype(v.dtype)

    state, hs = lax.scan(step, state, tuple(chunks(a) for a in (q, k, v, ig, lf)))
    return jnp.moveaxis(hs, 0, 2).reshape(b, h, t, d), state


def _hgrn_dir(q, k, v, lf, S):
    b, h, t, dk = q.shape
    dv = v.shape[-1]
    L = C_CHUNK
    nc = t // L
    mask = jnp.tril(jnp.ones((L, L), bool))[:, :, None]

    def chunks(a):
        return jnp.moveaxis(a.reshape(b, h, nc, L, a.shape[-1]), 2, 0)

    def step(S, inp):
        qc, kc, vc, fc = (a.astype(jnp.float32) for a in inp)
        cb = jnp.cumsum(fc, axis=2)
        decay = jnp.exp(jnp.where(mask, cb[:, :, :, None, :] - cb[:, :, None, :, :], NEG_BIG))
        scores = jnp.einsum('bhtc,bhsc,bhtsc->bhts', qc, kc, decay)
        o = jnp.einsum('bhts,bhse->bhte', scores, vc) + jnp.einsum('bhtc,bhce->bhte', qc * jnp.exp(cb), S)
        cl = cb[:, :, -1:, :]
        S_new = jnp.exp(cl[:, :, 0, :])[..., None] * S + jnp.einsum('bhsc,bhse->bhce', kc * jnp.exp(cl - cb), vc)
        return S_new, o.astype(q.dtype)

    S, os_ = lax.scan(step, S, tuple(chunks(a) for a in (q, k, v, lf)))
    return jnp.moveaxis(os_, 0, 2).reshape(b, h, t, dv), S


def _mlstm_branch(pc, pl, conv_w, norm_g, last):
    def prep(p):
        qk = jax.nn.silu(_conv_centred(jnp.concatenate([p['b_q'], p['b_k']], axis=-1), conv_w))
        q = _to_heads(qk[..., :B_WIDTH], B_HEADS)
        k = _to_heads(qk[..., B_WIDTH:], B_HEADS) * (B_HEAD_DIM ** -0.5)
        v = _to_heads(p['b_v'], B_HEADS)
        gate = lambda name: jnp.swapaxes(p[name].astype(jnp.float32), 1, 2)
        fl = lambda a: jnp.flip(a, 2)
        fwd = (q, k, v, gate('b_i_fwd'), jax.nn.log_sigmoid(gate('b_f_fwd')))
        bwd = (fl(q), fl(k), fl(v), fl(gate('b_i_bwd')), fl(jax.nn.log_sigmoid(gate('b_f_bwd'))))
        return fwd, bwd

    ctx_f, ctx_b = prep(pc)
    lat_f, lat_b = prep(pl)
    s0 = _mlstm_zero_state(pl['b_q'].shape[0])
    hcf, scf = _mlstm_dir(*ctx_f, s0)
    hcb, scb = _mlstm_dir(*ctx_b, s0)
    hlf, _ = _mlstm_dir(*lat_f, scf)
    hlb, _ = _mlstm_dir(*lat_b, scb)

    def readout(hf, hb_rev, p):
        hsum = _from_heads(hf + jnp.flip(hb_rev, 2))
        return _head_rmsnorm(hsum, norm_g, B_HEADS) * jax.nn.sigmoid(p['b_o']) * jax.nn.silu(p['b_z'])

    y_lat = readout(hlf, hlb, pl)
    y_ctx = None if last else readout(hcf, hcb, pc)
    return y_ctx, y_lat


def _hgrn_branch(pc, pl, rows, lb, norm_g, last):
    lb_h = lb.reshape(C_HEADS, 1, C_KEY_DIM)
    log_lb = jnp.log(jnp.maximum(lb_h, LB_FLOOR))
    log_1m_lb = jnp.log1p(-lb_h)

    def gate(logit):
        z = _to_heads(logit, C_HEADS).astype(jnp.float32)
        return (1.0 - lb_h) * jax.nn.sigmoid(-z), jnp.logaddexp(log_lb, log_1m_lb + jax.nn.log_sigmoid(z))

    def prep(p, order):
        q = _to_heads(jax.nn.silu(order(p['c_q'])), C_HEADS)
        v = _to_heads(order(p['c_i']), C_HEADS)
        kf, lff = gate(order(p['c_f_fwd']))
        kb, lfb = gate(order(p['c_f_bwd']))
        fl = lambda a: jnp.flip(a, 2)
        return (q, kf, v, lff), (fl(q), fl(kb), fl(v), fl(lfb))

    ident = lambda a: a
    ctx_f, ctx_b = prep(pc, ident)
    lat_f, lat_b = prep(pl, lambda a: _to_colmajor(a, rows))
    S0 = jnp.zeros((pl['c_q'].shape[0], C_HEADS, C_KEY_DIM, C_VAL_DIM), jnp.float32)
    ocf, scf = _hgrn_dir(*ctx_f, S0)
    ocb, scb = _hgrn_dir(*ctx_b, S0)
    olf, _ = _hgrn_dir(*lat_f, scf)
    olb, _ = _hgrn_dir(*lat_b, scb)

    def readout(of, ob_rev, p, unorder):
        o = unorder(_from_heads(of + jnp.flip(ob_rev, 2)))
        return _head_rmsnorm(o, norm_g, C_HEADS) * jax.nn.silu(p['c_g'])

    y_lat = readout(olf, olb, pl, lambda a: _from_colmajor(a, rows))
    y_ctx = None if last else readout(ocf, ocb, pc, ident)
    return y_ctx, y_lat


def _mixer(hc, hl, rows, w_in, b_in, w_s, b_s, conv_w, m_norm, lb, h_norm, w_out, last):
    pc = _split_cols(hc @ w_in + b_in)
    pl = _split_cols(hl @ w_in + b_in)
    ya_l = _chunk_mlp(pl['a_u'], pl['a_v'], pl['a_z'], w_s, b_s)
    yb_c, yb_l = _mlstm_branch(pc, pl, conv_w, m_norm, last)
    yc_c, yc_l = _hgrn_branch(pc, pl, rows, lb, h_norm, last)
    y_lat = jnp.concatenate([ya_l, yb_l, yc_l], axis=-1) @ w_out
    if last:
        return None, y_lat
    ya_c = _chunk_mlp(pc['a_u'], pc['a_v'], pc['a_z'], w_s, b_s)
    y_ctx = jnp.concatenate([ya_c, yb_c, yc_c], axis=-1) @ w_out
    return y_ctx, y_lat


def setup_inputs(seed: int = 0) -> dict:
    key = jax.random.key(seed)
    ks = jax.random.split(key, 20)
    nrm = jax.random.normal
    x = nrm(ks[0], (BATCH, SEQ, D_MODEL), jnp.float32)
    c = nrm(ks[1], (BATCH, D_MODEL), jnp.float32)
    ctx = nrm(ks[2], (BATCH, CTX_LEN, D_MODEL), jnp.float32)
    c_ctx = nrm(ks[3], (D_MODEL,), jnp.float32)
    w_ada = nrm(ks[4], (DEPTH, D_MODEL, 3 * D_MODEL), jnp.float32) * (0.5 * D_MODEL ** -0.5)
    b_ada = 0.02 * nrm(ks[5], (DEPTH, 3 * D_MODEL), jnp.float32)
    norm_g = 1.0 + 0.1 * nrm(ks[6], (DEPTH, D_MODEL), jnp.float32)
    w_in = nrm(ks[7], (DEPTH, D_MODEL, D_IN), jnp.float32) * (D_MODEL ** -0.5)
    b_in = 0.02 * nrm(ks[8], (DEPTH, D_IN), jnp.float32)
    f_off = jnp.linspace(FORGET_BIAS_LO, FORGET_BIAS_HI, B_HEADS, dtype=jnp.float32)
    for name in ('b_f_fwd', 'b_f_bwd'):
        s = _col_start(name)
        b_in = b_in.at[:, s:s + B_HEADS].add(f_off)
    w_spatial = nrm(ks[9], (DEPTH, A_GROUPS, A_CHUNK, A_CHUNK), jnp.float32) * (A_CHUNK ** -0.5)
    b_spatial = 1.0 + 0.1 * nrm(ks[10], (DEPTH, A_GROUPS, A_CHUNK), jnp.float32)
    conv_qk = nrm(ks[11], (DEPTH, CONV_W, 2 * B_WIDTH), jnp.float32) * (CONV_W ** -0.5)
    mlstm_norm = 1.0 + 0.1 * nrm(ks[12], (DEPTH, B_WIDTH), jnp.float32)
    hgrn_lb_logits = 0.1 * nrm(ks[13], (DEPTH, C_KEY_WIDTH), jnp.float32)
    hgrn_norm = 1.0 + 0.1 * nrm(ks[14], (DEPTH, C_WIDTH), jnp.float32)
    w_out = nrm(ks[15], (DEPTH, D_MIX, D_MODEL), jnp.float32) * (D_MIX ** -0.5)
    final_norm = 1.0 + 0.1 * nrm(ks[16], (D_MODEL,), jnp.float32)
    return {'x': x, 'c': c, 'ctx': ctx, 'c_ctx': c_ctx, 'w_ada': w_ada, 'b_ada': b_ada,
            'norm_g': norm_g, 'w_in': w_in, 'b_in': b_in, 'w_spatial': w_spatial,
            'b_spatial': b_spatial, 'conv_qk': conv_qk, 'mlstm_norm': mlstm_norm,
            'hgrn_lb_logits': hgrn_lb_logits, 'hgrn_norm': hgrn_norm, 'w_out': w_out,
            'final_norm': final_norm}


def reference(x, c, ctx, c_ctx, w_ada, b_ada, norm_g, w_in, b_in, w_spatial, b_spatial,
              conv_qk, mlstm_norm, hgrn_lb_logits, hgrn_norm, w_out, final_norm):
    rows = x.shape[1] // GRID_W
    p_lb = jax.nn.softmax(hgrn_lb_logits.astype(jnp.float32), axis=0)
    lower_bounds = jnp.cumsum(p_lb, axis=0) - p_lb[0]
    s_lat = jax.nn.silu(c)
    s_ctx = jax.nn.silu(c_ctx)
    h_lat, h_ctx = x, ctx
    for l in range(DEPTH):
        last = l == DEPTH - 1
        mod_l = (s_lat @ w_ada[l] + b_ada[l])[:, None, :]
        mod_c = s_ctx @ w_ada[l] + b_ada[l]
        sh_l, sc_l, g_l = jnp.split(mod_l, 3, axis=-1)
        sh_c, sc_c, g_c = jnp.split(mod_c, 3, axis=-1)
        n_lat = _rmsnorm(h_lat, norm_g[l]) * (1 + sc_l) + sh_l
        n_ctx = _rmsnorm(h_ctx, norm_g[l]) * (1 + sc_c) + sh_c
        y_ctx, y_lat = _mixer(n_ctx, n_lat, rows, w_in[l], b_in[l], w_spatial[l], b_spatial[l],
                              conv_qk[l], mlstm_norm[l], lower_bounds[l], hgrn_norm[l], w_out[l], last)
        h_lat = h_lat + g_l * y_lat
        if not last:
            h_ctx = h_ctx + g_c * y_ctx
    return _rmsnorm(h_lat, final_norm)
```

```python
import numpy as np
from contextlib import ExitStack
import concourse.bass as bass
import concourse.mybir as mybir
from concourse.bass_utils import run_bass_kernel_spmd

F32 = mybir.dt.float32
BF16 = mybir.dt.bfloat16
AF = mybir.ActivationFunctionType
ALU = mybir.AluOpType
AX = mybir.AxisListType

ENGS = ("pe", "act", "dve", "pool", "sp")
N_DSEM = 24

D = 2048
KD = 16
DIN = 9232
EPS = 1e-6
C_AU, C_AV, C_AZ = 0, 512, 1024
C_BQ, C_BK, C_BV, C_BO, C_BZ = 1536, 2560, 3584, 4608, 5632
C_GT = 6656
C_CQ, C_CFF, C_CFB, C_CI, C_CG = 6672, 7184, 7696, 8208, 8720

FM_STARTS = ([C_AU + i * 128 for i in range(4)] + [C_AZ + i * 128 for i in range(4)]
             + [C_BQ + i * 128 for i in range(8)] + [C_BK + i * 128 for i in range(8)]
             + [C_BO + i * 128 for i in range(8)] + [C_BZ + i * 128 for i in range(8)]
             + [C_CQ + i * 128 for i in range(4)] + [C_CFF + i * 128 for i in range(4)]
             + [C_CFB + i * 128 for i in range(4)] + [C_CG + i * 128 for i in range(4)])
FM_IDX = {c: i for i, c in enumerate(FM_STARTS)}
NFM = len(FM_STARTS)


class T:
    __slots__ = ("name", "lw", "rd")

    def __init__(self, name=""):
        self.name = name
        self.lw = None
        self.rd = []


class Prog:
    def __init__(self, nc):
        self.nc = nc
        self.ops = {e: [] for e in ENGS}
        self.known = {e: {o: -1 for o in ENGS} for e in ENGS}
        self.known_d = {e: [0] * N_DSEM for e in ENGS}
        self.dmas = []

    def _deps(self, eng, reads, writes):
        deps = []
        for t in reads:
            if t.lw is not None:
                deps.append(t.lw)
        for t in writes:
            if t.lw is not None:
                deps.append(t.lw)
            deps.extend(t.rd)
        best = {}
        dd = {}
        for d in deps:
            if d[0] == "d":
                slot, val = self.dmas[d[1]]
                if self.known_d[eng][slot] < val:
                    dd[slot] = max(dd.get(slot, 0), val)
            else:
                _, oe, idx = d
                if oe == eng and eng in ("pe", "sp"):
                    continue
                if self.known[eng][oe] < idx:
                    best[oe] = max(best.get(oe, -1), idx)
        waits = []
        for oe, idx in best.items():
            self.known[eng][oe] = idx
            self.ops[oe][idx][2] = True
            waits.append(("e", oe, idx))
        for slot, val in dd.items():
            self.known_d[eng][slot] = val
            waits.append(("d", slot, val))
        return waits

    def op(self, eng, fn, reads=(), writes=()):
        waits = self._deps(eng, reads, writes)
        idx = len(self.ops[eng])
        self.ops[eng].append([fn, waits, False, None])
        me = ("e", eng, idx)
        for t in reads:
            t.rd.append(me)
        for t in writes:
            t.lw = me
            t.rd = []
        return me

    def dma(self, q, out, in_, reads=(), writes=()):
        did = len(self.dmas)
        slot = did % N_DSEM
        val = 16 * (did // N_DSEM + 1)
        waits = self._deps(q, reads, writes)
        if val > 16 and self.known_d[q][slot] < val - 16:
            self.known_d[q][slot] = val - 16
            waits.append(("d", slot, val - 16))
        self.dmas.append((slot, val))
        fn = lambda e: e.dma_start(out=out, in_=in_)
        self.ops[q].append([fn, waits, False, did])
        me = ("d", did)
        for t in reads:
            t.rd.append(me)
        for t in writes:
            t.lw = me
            t.rd = []
        return me

    def barrier(self):
        last = {e: len(self.ops[e]) - 1 for e in ENGS}
        ndma = len(self.dmas)
        for e in ENGS:
            waits = []
            for oe in ENGS:
                if oe == e:
                    continue
                idx = last[oe]
                while idx >= 0 and (self.ops[oe][idx][3] is not None or self.ops[oe][idx][0] is None):
                    idx -= 1
                if idx >= 0 and self.known[e][oe] < idx:
                    self.known[e][oe] = idx
                    self.ops[oe][idx][2] = True
                    waits.append(("e", oe, idx))
            for did in range(max(0, ndma - N_DSEM), ndma):
                slot, val = self.dmas[did]
                if self.known_d[e][slot] < val:
                    self.known_d[e][slot] = val
                    waits.append(("d", slot, val))
            self.ops[e].append([None, waits, False, None])

    def emit(self):
        nc = self.nc
        with ExitStack() as es:
            esem = {e: es.enter_context(nc.semaphore("s_" + e)) for e in ENGS}
            dsem = [es.enter_context(nc.semaphore("d%d" % i)) for i in range(N_DSEM)]
            val = {}
            for e in ENGS:
                c = 0
                for i, o in enumerate(self.ops[e]):
                    if o[2]:
                        c += 1
                        val[(e, i)] = c
            block = es.enter_context(nc.Block())

            def run(e, eng):
                for i, (fn, waits, marked, did) in enumerate(self.ops[e]):
                    for w in waits:
                        if w[0] == "e":
                            eng.wait_ge(esem[w[1]], val[(w[1], w[2])])
                        else:
                            eng.wait_ge(dsem[w[1]], w[2])
                    if fn is None:
                        continue
                    ins = fn(eng)
                    if did is not None:
                        ins.then_inc(dsem[self.dmas[did][0]], 16)
                    elif marked:
                        ins.then_inc(esem[e], 1)

            @block.tensor
            def _(eng):
                run("pe", eng)

            @block.scalar
            def _(eng):
                run("act", eng)

            @block.vector
            def _(eng):
                run("dve", eng)

            @block.gpsimd
            def _(eng):
                run("pool", eng)

            @block.sync
            def _(eng):
                run("sp", eng)


class Buf:
    def __init__(self, t, name=""):
        self.t = t
        self.T = T(name)

    def __getitem__(self, k):
        return self.t[k]


class Ring:
    def __init__(self, bufs):
        self.bufs = bufs
        self.i = 0

    def next(self):
        b = self.bufs[self.i % len(self.bufs)]
        self.i += 1
        return b


def fap(ap, dims, off=0):
    a = ap.ap
    return bass.AP(ap.tensor, ap.offset + off, [list(a[0])] + [list(d) for d in dims])


def build(TL=4096, TC=256, DEPTH=4, dbg=False, stop_after=None):
    nc = bass.Bass("TRN2", target_bir_lowering=False)
    ROWS = TL // 64
    TT = TC + TL
    P = Prog(nc)

    def din(name, shape, dt=F32):
        return nc.dram_tensor(name, list(shape), dt, kind="ExternalInput").ap()

    def dscr(name, shape, dt=F32):
        return nc.dram_tensor(name, list(shape), dt, kind="ExternalOutput" if dbg else "Internal").ap()

    xT = din("xT", [D, TL])
    cxT = din("cxT", [D, TC])
    cs_d = din("cs", [128, KD, 2])
    w_ada = din("w_ada", [DEPTH, D, 3 * D])
    badaT = din("badaT", [128, DEPTH, 48])
    normgT = din("normgT", [128, DEPTH, KD])
    w_in = din("w_in", [DEPTH, D, DIN])
    b_in = din("b_in", [DEPTH, DIN])
    bfm_d = din("bfm", [128, DEPTH, NFM])
    wsT_d = din("wsT", [DEPTH, 128, 4, 128])
    bs_d = din("bs", [DEPTH, 512])
    convT_d = din("convT", [128, DEPTH, 16, 3])
    mnT_d = din("mnT", [128, DEPTH, 8])
    hnT_d = din("hnT", [128, DEPTH, 4])
    lbT_d = din("lbT", [128, 4, DEPTH])
    fnT_d = din("fnT", [128, KD])
    w_out = din("w_out", [DEPTH, D, D])
    outT = nc.dram_tensor("outT", [D, TL], F32, kind="ExternalOutput").ap()

    hT = {0: dscr("hlT", [D, TL]), 1: dscr("hcT", [D, TC])}
    uzT = dscr("uzT", [512, TT], BF16)
    qpT = dscr("qpT", [1024, TT], BF16)
    kpT = dscr("kpT", [1024, TT], BF16)
    gBT = dscr("gBT", [1024, TT], BF16)
    vn_d = dscr("vn", [TT, 512], BF16)
    vb_d = dscr("vb", [TT, 1024], BF16)
    gts_d = dscr("gts", [TT, 16], F32)
    cqT = dscr("cqT", [512, TT], BF16)
    ckT = [dscr("ckT%d" % i, [512, TT], BF16) for i in range(2)]
    lfT = [dscr("lfT%d" % i, [512, TT], F32) for i in range(2)]
    gCT = dscr("gCT", [512, TT], BF16)
    ci_d = dscr("ci", [TT, 512], BF16)
    ymT = dscr("ymT", [1536, TT], BF16)
    qcT = dscr("qcT", [1024, TT], BF16)
    kcT = dscr("kcT", [1024, TT], BF16)
    ktm_d = dscr("ktm", [TT, 1024], BF16)
    hf_d = dscr("hf", [TT, 1024], F32)
    hbk_d = dscr("hbk", [TT, 1024], F32)
    ofT = dscr("ofT", [512, TT], F32)

    with ExitStack() as glob:
        uid = [0]

        def sb(es, name, shape, dt=F32):
            uid[0] += 1
            return Buf(es.enter_context(nc.sbuf_tensor("sb%d_%s" % (uid[0], name), list(shape), dt)), name)

        def ps(es, name, shape, dt=F32):
            uid[0] += 1
            return Buf(es.enter_context(nc.psum_tensor("ps%d_%s" % (uid[0], name), list(shape), dt)), name)

        ident = sb(glob, "ident", [128, 128])
        ones_bf = sb(glob, "ones_bf", [128, 128], BF16)
        ones_f = sb(glob, "ones_f", [128, 128])
        tri_f = sb(glob, "tri_f", [128, 128])
        tri_b = sb(glob, "tri_b", [128, 128])
        epsc = sb(glob, "epsc", [128, 1])
        modT = sb(glob, "modT", [128, DEPTH, 2, 48])
        gsT = sb(glob, "gsT", [128, DEPTH, 2, KD])
        normg = sb(glob, "normg", [128, DEPTH, KD])
        bfm = sb(glob, "bfm", [128, DEPTH, NFM])
        convT = sb(glob, "convT", [128, DEPTH, 16, 3])
        mnT = sb(glob, "mnT", [128, DEPTH, 8])
        hnT = sb(glob, "hnT", [128, DEPTH, 4])
        fnT = sb(glob, "fnT", [128, KD])
        lb = sb(glob, "lb", [128, 4, DEPTH])
        oml = sb(glob, "oml", [128, 4, DEPTH])
        noml = sb(glob, "noml", [128, 4, DEPTH])
        psb = [ps(glob, "psb%d" % i, [128, 512]) for i in range(8)]
        css = sb(glob, "css", [128, KD, 2])
        bada = sb(glob, "bada", [128, DEPTH, 48])

        def act(fn, reads, writes):
            return P.op("act", fn, [b.T for b in reads], [b.T for b in writes])

        def dve(fn, reads, writes):
            return P.op("dve", fn, [b.T for b in reads], [b.T for b in writes])

        def pool(fn, reads, writes):
            return P.op("pool", fn, [b.T for b in reads], [b.T for b in writes])

        def pe(fn, reads, writes):
            return P.op("pe", fn, [b.T for b in reads], [b.T for b in writes])

        def dma(q, out, in_, reads=(), writes=()):
            return P.dma(q, out, in_, [b.T for b in reads], [b.T for b in writes])

        def mm(out, lhsT, rhs, start, stop, reads, writes):
            pe(lambda e: e.matmul(out, lhsT=lhsT, rhs=rhs, start=start, stop=stop), reads, writes)

        pool(lambda e: e.memset(ident[:], 0.0), [], [ident])
        pool(lambda e: e.affine_select(out=ident[:], in_=ident[:], pattern=[[-1, 128]], compare_op=ALU.not_equal,
                                       fill=1.0, base=0, channel_multiplier=1), [ident], [ident])
        pool(lambda e: e.memset(ones_bf[:], 1.0), [], [ones_bf])
        pool(lambda e: e.memset(ones_f[:], 1.0), [], [ones_f])
        pool(lambda e: e.memset(tri_f[:], 1.0), [], [tri_f])
        pool(lambda e: e.affine_select(out=tri_f[:], in_=tri_f[:], pattern=[[1, 128]], compare_op=ALU.is_ge,
                                       fill=0.0, base=0, channel_multiplier=-1), [tri_f], [tri_f])
        pool(lambda e: e.memset(tri_b[:], 1.0), [], [tri_b])
        pool(lambda e: e.affine_select(out=tri_b[:], in_=tri_b[:], pattern=[[-1, 128]], compare_op=ALU.is_ge,
                                       fill=0.0, base=0, channel_multiplier=1), [tri_b], [tri_b])
        pool(lambda e: e.memset(epsc[:], EPS), [], [epsc])
        dma("sp", normg[:], normgT, [], [normg])
        dma("sp", bfm[:], bfm_d, [], [bfm])
        dma("sp", convT[:], convT_d, [], [convT])
        dma("sp", mnT[:], mnT_d, [], [mnT])
        dma("sp", hnT[:], hnT_d, [], [hnT])
        dma("sp", fnT[:], fnT_d, [], [fnT])

        with ExitStack() as es:
            lgt = sb(es, "lgt", [128, 4, DEPTH])
            lsum = sb(es, "lsum", [128, 4])
            dma("sp", css[:], cs_d, [], [css])
            dma("sp", bada[:], badaT, [], [bada])
            dma("sp", lgt[:], lbT_d, [], [lgt])
            act(lambda e: e.activation(out=css[:], in_=css[:], func=AF.Silu), [css], [css])
            act(lambda e: e.activation(out=lgt[:], in_=lgt[:], func=AF.Exp), [lgt], [lgt])
            dve(lambda e: e.tensor_reduce(out=lsum[:], in_=lgt[:], axis=AX.X, op=ALU.add), [lgt], [lsum])
            dve(lambda e: e.reciprocal(out=lsum[:], in_=lsum[:]), [lsum], [lsum])
            dve(lambda e: e.tensor_tensor(out=lgt[:], in0=lgt[:], in1=fap(lsum[:], [[1, 4], [0, DEPTH]]),
                                          op=ALU.mult), [lgt, lsum], [lgt])
            dve(lambda e: e.memset(lb[:, :, 0:1], 0.0), [], [lb])
            for l in range(1, DEPTH):
                dve(lambda e, l=l: e.tensor_tensor(out=lb[:, :, l:l + 1], in0=lb[:, :, l - 1:l], in1=lgt[:, :, l:l + 1],
                                                   op=ALU.add), [lb, lgt], [lb])
            dve(lambda e: e.tensor_scalar(out=oml[:], in0=lb[:], scalar1=-1.0, scalar2=1.0, op0=ALU.mult, op1=ALU.add),
                [lb], [oml])
            dve(lambda e: e.tensor_scalar(out=noml[:], in0=lb[:], scalar1=1.0, scalar2=-1.0, op0=ALU.mult, op1=ALU.add),
                [lb], [noml])
            pm = psb[0]
            P.barrier()

        def compute_mod(l, es):
            wa = [sb(es, "wa%d_%d" % (l, i), [128, KD, 512]) for i in range(2)]
            for jg in range(12):
                w = wa[jg % 2]
                src = w_ada[l, :, jg * 512:(jg + 1) * 512].rearrange("(k p) f -> p k f", p=128)
                dma("sp" if jg % 2 == 0 else "act", w[:], src, [], [w])
                for jj in range(4):
                    j = jg * 4 + jj
                    for k in range(KD):
                        mm(pm[:, j * 2:(j + 1) * 2], w[:, k, jj * 128:(jj + 1) * 128], css[:, k, :],
                           k == 0, k == KD - 1, [w, css], [pm])
                yield
            dve(lambda e: e.tensor_tensor(
                out=fap(modT[:], [[1, 48], [48, 2]], off=l * 96),
                in0=fap(pm[:], [[2, 48], [1, 2]]),
                in1=fap(bada[:], [[1, 48], [0, 2]], off=l * 48), op=ALU.add), [pm, bada], [modT])
            for s_ in range(2):
                dve(lambda e, s_=s_: e.scalar_tensor_tensor(
                    out=gsT[:, l, s_, :], in0=modT[:, l, s_, 16:32], scalar=1.0, in1=normg[:, l, :],
                    op0=ALU.add, op1=ALU.mult), [modT, normg], [gsT])

        with ExitStack() as es0:
            for _ in compute_mod(0, es0):
                pass
            P.barrier()

        if dbg:
            dbg_mod = nc.dram_tensor("dbg_mod", [128, DEPTH * 96], F32, kind="ExternalOutput").ap()
            dbg_lb = nc.dram_tensor("dbg_lb", [128, 4 * DEPTH], F32, kind="ExternalOutput").ap()
            dma("sp", dbg_mod, modT[:].rearrange("p l s j -> p (l s j)"), [modT], [])
            dma("sp", dbg_lb, lb[:].rearrange("p c l -> p (c l)"), [lb], [])

        def blocks(s):
            if s == 1:
                return [(t0, min(512, TC - t0)) for t0 in range(0, TC, 512)]
            return [(t0, 512) for t0 in range(0, TL, 512)]
        GOFF = {1: 0, 0: TC}
        all_blocks = [(1, t0, n) for (t0, n) in blocks(1)] + [(0, t0, n) for (t0, n) in blocks(0)]

        def layer(l):
            last = l == DEPTH - 1
            src_h = {0: xT, 1: cxT} if l == 0 else hT
            with ExitStack() as lay:
                nT = sb(lay, "nT", [128, KD + 1, TT], BF16)
                slotT = [Buf(None, "slot%d" % i) for i in range(KD + 1)]
                phys = list(range(KD))
                spare = [KD]
                with ExitStack() as es:
                    hin = [sb(es, "hin%d" % i, [128, KD, 256]) for i in range(2)]
                    sq = sb(es, "sq", [128, KD, 256], BF16)
                    rstd = sb(es, "rstd", [128, 256])
                    ut = [sb(es, "ut%d" % i, [128, 256]) for i in range(4)]
                    nblocks = [(s_, t0_ + o_, min(256, n_ - o_)) for (s_, t0_, n_) in all_blocks for o_ in range(0, n_, 256)]
                    for bi, (s, t0, n) in enumerate(nblocks):
                        h = hin[bi % 2]
                        dma("sp", h[:, :, 0:n], src_h[s][:, t0:t0 + n].rearrange("(k p) t -> p k t", p=128), [], [h])
                        act(lambda e, h=h, n=n: e.activation(out=sq[:, :, 0:n], in_=h[:, :, 0:n], func=AF.Square), [h], [sq])
                        pss = psb[bi % 2]
                        for k in range(KD):
                            mm(pss[:, 0:n], ones_bf[:], sq[:, k, 0:n], k == 0, k == KD - 1, [ones_bf, sq], [pss])
                        act(lambda e, pss=pss, n=n: e.activation(out=rstd[:, 0:n], in_=pss[:, 0:n], func=AF.Sqrt,
                                                                 bias=epsc[:], scale=1.0 / D), [pss, epsc], [rstd])
                        dve(lambda e, n=n: e.reciprocal(out=rstd[:, 0:n], in_=rstd[:, 0:n]), [rstd], [rstd])
                        g0 = GOFF[s] + t0
                        for k in range(KD):
                            u = ut[k % 4]
                            dve(lambda e, u=u, h=h, k=k, n=n: e.tensor_tensor(out=u[:, 0:n], in0=h[:, k, 0:n], in1=rstd[:, 0:n],
                                                                              op=ALU.mult), [h, rstd], [u])
                            if k % 2 == 0:
                                act(lambda e, u=u, k=k, n=n, g0=g0, s=s: e.activation(
                                    out=nT[:, k, g0:g0 + n], in_=u[:, 0:n], func=AF.Identity,
                                    bias=modT[:, l, s, k:k + 1], scale=gsT[:, l, s, k:k + 1]), [u, modT, gsT], [slotT[k]])
                            else:
                                pool(lambda e, u=u, k=k, n=n, g0=g0, s=s: e.tensor_scalar(
                                    out=nT[:, k, g0:g0 + n], in0=u[:, 0:n], scalar1=gsT[:, l, s, k:k + 1],
                                    scalar2=modT[:, l, s, k:k + 1], op0=ALU.mult, op1=ALU.add), [u, modT, gsT], [slotT[k]])
                    P.barrier()
                if dbg and l == 0:
                    dbg_nT = nc.dram_tensor("dbg_nT", [128, KD, TT], BF16, kind="ExternalOutput").ap()
                    dma("sp", dbg_nT, nT[:, 0:KD, :], slotT, [])
                if stop_after == "N":
                    return True

                def n_nat(k, s, t0, n):
                    g0 = GOFF[s] + t0
                    return nT[:, phys[k], g0:g0 + n]

                n_scan = n_nat

                with ExitStack() as es:
                    wt = Ring([sb(es, "wt%d" % i, [128, KD, 256], BF16) for i in range(3)])
                    st32 = Ring([sb(es, "st32_%d" % i, [128, 512]) for i in range(2)])
                    st16 = Ring([sb(es, "st16_%d" % i, [128, 512], BF16) for i in range(4)])
                    tmp32 = Ring([sb(es, "tmp32_%d" % i, [128, 512]) for i in range(3)])
                    bias_tm = sb(es, "bias_tm", [128, 2064])
                    stat = Ring([sb(es, "stat%d" % i, [128, 8]) for i in range(4)])
                    pring = Ring(psb)
                    tm_off = {C_AV: 0, C_BV: 512, C_GT: 1536, C_CI: 1552}
                    for c0, n in ((C_AV, 512), (C_BV, 1024), (C_GT, 16), (C_CI, 512)):
                        dma("sp", bias_tm[:, tm_off[c0]:tm_off[c0] + n],
                            b_in[l:l + 1, c0:c0 + n].partition_broadcast(128), [], [bias_tm])

                    def load_w(c0, n):
                        w = wt.next()
                        dma("pool", w[:, :, 0:n], w_in[l, :, c0:c0 + n].rearrange("(k p) f -> p k f", p=128), [], [w])
                        return w

                    def fm_mm(w, sub, nfun, s, t0, n):
                        pb = pring.next()
                        for k in range(KD):
                            mm(pb[:, 0:n], w[:, k, sub * 128:(sub + 1) * 128], nfun(k, s, t0, n), k == 0, k == KD - 1,
                               [w, slotT[phys[k]]], [pb])
                        return pb

                    def bias_ap(c0):
                        return bfm[:, l, FM_IDX[c0]:FM_IDX[c0] + 1]

                    def store(q, dst, src_ap, buf):
                        dma(q, dst, src_ap, [buf], [])

                    for j in range(2):
                        wu = load_w(C_AU + j * 256, 256)
                        wz = load_w(C_AZ + j * 256, 256)
                        for sub in range(2):
                            f0 = j * 256 + sub * 128
                            for (s, t0, n) in all_blocks:
                                if last and s == 1:
                                    continue
                                g0 = GOFF[s] + t0
                                pz = fm_mm(wz, sub, n_nat, s, t0, n)
                                pu = fm_mm(wu, sub, n_nat, s, t0, n)
                                zs = tmp32.next()
                                act(lambda e, zs=zs, pz=pz, n=n, f0=f0: e.activation(
                                    out=zs[:, 0:n], in_=pz[:, 0:n], func=AF.Silu, bias=bias_ap(C_AZ + f0)), [pz, bfm], [zs])
                                o = st16.next()
                                dve(lambda e, o=o, pu=pu, zs=zs, n=n, f0=f0: e.scalar_tensor_tensor(
                                    out=o[:, 0:n], in0=pu[:, 0:n], scalar=bias_ap(C_AU + f0), in1=zs[:, 0:n],
                                    op0=ALU.add, op1=ALU.mult), [pu, zs, bfm], [o])
                                store("sp", uzT[f0:f0 + 128, g0:g0 + n], o[:, 0:n], o)
                    def fm_simple(c_base, ncols, dst, func, nfun, skip_ctx_last=False):
                        for j in range(ncols // 256):
                            w = load_w(c_base + j * 256, 256)
                            for sub in range(2):
                                f0 = j * 256 + sub * 128
                                for (s, t0, n) in all_blocks:
                                    g0 = GOFF[s] + t0
                                    pb = fm_mm(w, sub, nfun, s, t0, n)
                                    o = st16.next()
                                    act(lambda e, o=o, pb=pb, n=n, f0=f0: e.activation(
                                        out=o[:, 0:n], in_=pb[:, 0:n], func=func, bias=bias_ap(c_base + f0)), [pb, bfm], [o])
                                    store("sp", dst[f0:f0 + 128, g0:g0 + n], o[:, 0:n], o)
                    fm_simple(C_BQ, 1024, qpT, AF.Identity, n_nat)
                    fm_simple(C_BK, 1024, kpT, AF.Identity, n_nat)
                    for j in range(4):
                        wo = load_w(C_BO + j * 256, 256)
                        wz = load_w(C_BZ + j * 256, 256)
                        for sub in range(2):
                            f0 = j * 256 + sub * 128
                            for (s, t0, n) in all_blocks:
                                if last and s == 1:
                                    continue
                                g0 = GOFF[s] + t0
                                po = fm_mm(wo, sub, n_nat, s, t0, n)
                                pz = fm_mm(wz, sub, n_nat, s, t0, n)
                                so = tmp32.next()
                                sz = tmp32.next()
                                act(lambda e, so=so, po=po, n=n, f0=f0: e.activation(
                                    out=so[:, 0:n], in_=po[:, 0:n], func=AF.Sigmoid, bias=bias_ap(C_BO + f0)), [po, bfm], [so])
                                act(lambda e, sz=sz, pz=pz, n=n, f0=f0: e.activation(
                                    out=sz[:, 0:n], in_=pz[:, 0:n], func=AF.Silu, bias=bias_ap(C_BZ + f0)), [pz, bfm], [sz])
                                o = st16.next()
                                pool(lambda e, o=o, so=so, sz=sz, n=n: e.tensor_tensor(
                                    out=o[:, 0:n], in0=so[:, 0:n], in1=sz[:, 0:n], op=ALU.mult), [so, sz], [o])
                                store("sp", gBT[f0:f0 + 128, g0:g0 + n], o[:, 0:n], o)
                    def tm_tiles(scan):
                        res = []
                        for (s, t0, n) in all_blocks:
                            for tt in range(0, n, 128):
                                res.append((s, t0 + tt))
                        return res

                    def tm_mm(w, ncols, nfun, s, t0):
                        pb = pring.next()
                        for k in range(KD):
                            lhs = nfun(k, s, t0, 128)
                            mm(pb[:, 0:ncols], lhs, w[:, k, 0:ncols], k == 0, k == KD - 1, [w, slotT[phys[k]]], [pb])
                        return pb

                    for j in range(2):
                        w = load_w(C_AV + j * 256, 256)
                        for (s, t0) in tm_tiles(False):
                            if last and s == 1:
                                continue
                            g0 = GOFF[s] + t0
                            pb = tm_mm(w, 256, n_nat, s, t0)
                            v = tmp32.next()
                            dve(lambda e, v=v, pb=pb, j=j: e.tensor_tensor(
                                out=v[:, 0:256], in0=pb[:, 0:256], in1=bias_tm[:, j * 256:(j + 1) * 256], op=ALU.add),
                                [pb, bias_tm], [v])
                            o = st16.next()
                            for g in range(2):
                                stt = stat.next()
                                dve(lambda e, stt=stt, v=v, g=g: e.bn_stats(out=stt[:, 0:6], in_=v[:, g * 128:(g + 1) * 128]), [v], [stt])
                                dve(lambda e, stt=stt: e.bn_aggr(out=stt[:, 6:8], in_=stt[:, 0:6]), [stt], [stt])
                                act(lambda e, stt=stt: e.activation(out=stt[:, 7:8], in_=stt[:, 7:8], func=AF.Sqrt, bias=epsc[:], scale=1.0),
                                    [stt, epsc], [stt])
                                dve(lambda e, stt=stt: e.reciprocal(out=stt[:, 7:8], in_=stt[:, 7:8]), [stt], [stt])
                                dve(lambda e, stt=stt, v=v, g=g, o=o: e.tensor_scalar(
                                    out=o[:, g * 128:(g + 1) * 128], in0=v[:, g * 128:(g + 1) * 128], scalar1=stt[:, 6:7],
                                    scalar2=stt[:, 7:8], op0=ALU.subtract, op1=ALU.mult), [stt, v], [o])
                            store("sp", vn_d[g0:g0 + 128, j * 256:(j + 1) * 256], o[:, 0:256], o)

                    def tm_simple(c_base, ncols_total, dst, dt, nfun, boff):
                        step = min(256, ncols_total)
                        for j in range(ncols_total // step):
                            w = load_w(c_base + j * step, step)
                            for (s, t0) in tm_tiles(False):
                                g0 = GOFF[s] + t0
                                pb = tm_mm(w, step, nfun, s, t0)
                                o = st16.next() if dt == BF16 else st32.next()
                                dve(lambda e, o=o, pb=pb, j=j: e.tensor_tensor(
                                    out=o[:, 0:step], in0=pb[:, 0:step], in1=bias_tm[:, boff + j * step:boff + (j + 1) * step],
                                    op=ALU.add), [pb, bias_tm], [o])
                                store("sp", dst[g0:g0 + 128, j * step:(j + 1) * step], o[:, 0:step], o)
                    tm_simple(C_BV, 1024, vb_d, BF16, n_nat, 512)
                    tm_simple(C_GT, 16, gts_d, F32, n_nat, 1536)
                    for k in range(KD):
                        src_slot, dst_slot = phys[k], spare[0]
                        engs = ("dve", "pool", "act")
                        eng = engs[k % 3]
                        src_ap = nT[:, src_slot, TC:TT].rearrange("p (r c) -> p c r", c=64)
                        dst_ap = nT[:, dst_slot, TC:TT].rearrange("p (c r) -> p c r", r=ROWS)
                        if eng == "act":
                            fn = lambda e, o=dst_ap, i=src_ap: e.copy(out=o, in_=i)
                            fn2 = lambda e, o=nT[:, dst_slot, 0:TC], i=nT[:, src_slot, 0:TC]: e.copy(out=o, in_=i)
                        else:
                            fn = lambda e, o=dst_ap, i=src_ap: e.tensor_copy(out=o, in_=i)
                            fn2 = lambda e, o=nT[:, dst_slot, 0:TC], i=nT[:, src_slot, 0:TC]: e.tensor_copy(out=o, in_=i)
                        P.op(eng, fn, [slotT[src_slot].T], [slotT[dst_slot].T])
                        P.op(eng, fn2, [slotT[src_slot].T], [slotT[dst_slot].T])
                        phys[k] = dst_slot
                        spare[0] = src_slot
                    fm_simple(C_CQ, 512, cqT, AF.Silu, n_scan)
                    if True:
                        fm_simple(C_CG, 512, gCT, AF.Silu, n_scan)
                    for d, cb in enumerate((C_CFF, C_CFB)):
                        for j in range(2):
                            w = load_w(cb + j * 256, 256)
                            for sub in range(2):
                                f0 = j * 256 + sub * 128
                                ch = f0 // 128
                                for (s, t0, n) in all_blocks:
                                    g0 = GOFF[s] + t0
                                    pb = fm_mm(w, sub, n_scan, s, t0, n)
                                    sg = tmp32.next()
                                    act(lambda e, sg=sg, pb=pb, n=n, f0=f0, cb=cb: e.activation(
                                        out=sg[:, 0:n], in_=pb[:, 0:n], func=AF.Sigmoid, bias=bias_ap(cb + f0)), [pb, bfm], [sg])
                                    ko = st16.next()
                                    pool(lambda e, ko=ko, sg=sg, n=n, ch=ch: e.tensor_scalar(
                                        out=ko[:, 0:n], in0=sg[:, 0:n], scalar1=noml[:, ch, l:l + 1], scalar2=oml[:, ch, l:l + 1],
                                        op0=ALU.mult, op1=ALU.add), [sg, noml, oml], [ko])
                                    store("sp", ckT[d][f0:f0 + 128, g0:g0 + n], ko[:, 0:n], ko)
                                    lo = st32.next()
                                    act(lambda e, lo=lo, sg=sg, n=n, ch=ch: e.activation(
                                        out=lo[:, 0:n], in_=sg[:, 0:n], func=AF.Ln, bias=lb[:, ch, l:l + 1],
                                        scale=oml[:, ch, l:l + 1]), [sg, lb, oml], [lo])
                                    store("sp", lfT[d][f0:f0 + 128, g0:g0 + n], lo[:, 0:n], lo)
                    tm_simple(C_CI, 512, ci_d, BF16, n_scan, 1552)
                    P.barrier()
            if stop_after == "P":
                return True
            GT = lambda s: GOFF[s]
            mask_f, mask_b = tri_f, tri_b

            with ExitStack() as es:
                wsb = sb(es, "wsb", [128, 4, 128], BF16)
                bsr = sb(es, "bsr", [1, 512])
                vt = Ring([sb(es, "vt%d" % i, [128, 512], BF16) for i in range(3)])
                uzr = Ring([sb(es, "uzr%d" % i, [128, 4, 512], BF16) for i in range(2)])
                yar = Ring([sb(es, "yar%d" % i, [128, 4, 512], BF16) for i in range(2)])
                pring = Ring(psb)
                dma("pool", wsb[:], wsT_d[l], [], [wsb])
                dma("sp", bsr[:], bs_d[l:l + 1, :], [], [bsr])
                for (s, t0, n) in all_blocks:
                    if last and s == 1:
                        continue
                    g0 = GOFF[s] + t0
                    uz = uzr.next()
                    ya = yar.next()
                    dma("act", uz[:, :, 0:n], uzT[:, g0:g0 + n].rearrange("(g p) t -> p g t", p=128), [], [uz])
                    for c in range(n // 128):
                        v = vt.next()
                        dma("sp", v[:], vn_d[g0 + c * 128:g0 + (c + 1) * 128, :], [], [v])
                        pb = pring.next()
                        for g in range(4):
                            mm(pb[:, g * 128:(g + 1) * 128], v[:, g * 128:(g + 1) * 128], wsb[:, g, :], True, False, [v, wsb], [pb])
                            mm(pb[:, g * 128:(g + 1) * 128], ones_f[0:1, :], bsr[0:1, g * 128:(g + 1) * 128], False, True,
                               [ones_f, bsr], [pb])
                        dve(lambda e, ya=ya, pb=pb, uz=uz, c=c: e.tensor_tensor(
                            out=ya[:, :, c * 128:(c + 1) * 128], in0=pb[:, :].rearrange("p (g t) -> p g t", g=4),
                            in1=uz[:, :, c * 128:(c + 1) * 128], op=ALU.mult), [pb, uz], [ya])
                    dma("sp", ymT[0:512, g0:g0 + n].rearrange("(g p) t -> p g t", p=128), ya[:, :, 0:n], [ya], [])
                P.barrier()
            if stop_after == "A":
                return True

            with ExitStack() as es:
                xin = Ring([sb(es, "xin%d" % i, [128, 8, 514], BF16) for i in range(2)])
                acc = Ring([sb(es, "acc%d" % i, [128, 512]) for i in range(3)])
                xo = Ring([sb(es, "xo%d" % i, [128, 8, 512], BF16) for i in range(2)])
                kt = Ring([sb(es, "kt%d" % i, [128, 4, 1024], BF16) for i in range(2)])
                ident_bf = sb(es, "ident_bf", [128, 128], BF16)
                pool(lambda e: e.tensor_copy(out=ident_bf[:], in_=ident[:]), [ident], [ident_bf])
                pring = Ring(psb[1:8])
                modgen = compute_mod(l + 1, es) if not last else iter(())
                for (s, t0, n) in all_blocks:
                    g0 = GOFF[s] + t0
                    ns = TC if s == 1 else TL
                    for qk, (src, dst) in enumerate(((qpT, qcT), (kpT, kcT))):
                        x = xin.next()
                        lo = 1 if t0 > 0 else 0
                        hi = 1 if t0 + n < ns else 0
                        if not lo:
                            pool(lambda e, x=x: e.memset(x[:, :, 0:1], 0.0), [], [x])
                        if not hi:
                            pool(lambda e, x=x, n=n: e.memset(x[:, :, n + 1:n + 2], 0.0), [], [x])
                        dma("sp" if qk == 0 else "act", x[:, :, 1 - lo:n + 1 + hi],
                            src[:, g0 - lo:g0 + n + hi].rearrange("(c p) t -> p c t", p=128), [], [x])
                        o = xo.next()
                        for c in range(8):
                            a = acc.next()
                            cc = qk * 8 + c
                            act(lambda e, a=a, x=x, c=c, cc=cc, n=n: e.activation(
                                out=a[:, 0:n], in_=x[:, c, 0:n], func=AF.Identity, scale=convT[:, l, cc, 0:1]),
                                [x, convT], [a])
                            dve(lambda e, a=a, x=x, c=c, cc=cc, n=n: e.scalar_tensor_tensor(
                                out=a[:, 0:n], in0=x[:, c, 1:n + 1], scalar=convT[:, l, cc, 1:2], in1=a[:, 0:n],
                                op0=ALU.mult, op1=ALU.add), [x, convT, a], [a])
                            dve(lambda e, a=a, x=x, c=c, cc=cc, n=n: e.scalar_tensor_tensor(
                                out=a[:, 0:n], in0=x[:, c, 2:n + 2], scalar=convT[:, l, cc, 2:3], in1=a[:, 0:n],
                                op0=ALU.mult, op1=ALU.add), [x, convT, a], [a])
                            act(lambda e, a=a, o=o, c=c, n=n: e.activation(out=o[:, c, 0:n], in_=a[:, 0:n], func=AF.Silu), [a], [o])
                        dma("sp", dst[:, g0:g0 + n].rearrange("(c p) t -> p c t", p=128), o[:, :, 0:n], [o], [])
                        if qk == 1:
                            ktb = kt.next()
                            for cch in range(n // 128):
                                for half in range(2):
                                    pb = pring.next()
                                    pbv = pb[:, :].bitcast(BF16)
                                    for c4 in range(4):
                                        c = half * 4 + c4
                                        pe(lambda e, pbv=pbv, o=o, c=c, c4=c4, cch=cch: e.transpose(
                                            out=pbv[:, c4 * 128:(c4 + 1) * 128], in_=o[:, c, cch * 128:(cch + 1) * 128],
                                            identity=ident_bf[:]), [o, ident_bf], [pb])
                                    act(lambda e, ktb=ktb, pbv=pbv, cch=cch, half=half: e.copy(
                                        out=ktb[:, cch, half * 512:(half + 1) * 512], in_=pbv[:, 0:512]), [pb], [ktb])
                            dma("act", ktm_d[g0:g0 + n, :].rearrange("(c p) f -> p c f", p=128), ktb[:, 0:n // 128, :], [ktb], [])
                        next(modgen, None)
                for _ in modgen:
                    pass
                P.barrier()
            if stop_after == "B1":
                return True

            LN16 = float(np.log(16.0))
            with ExitStack() as es:
                nchs = {1: TC // 128, 0: TL // 128}
                G = {s: sb(es, "G%d" % s, [128, nchs[s], 16]) for s in (0, 1)}
                NLF = {s: sb(es, "NLF%d" % s, [128, nchs[s], 2, 4]) for s in (0, 1)}
                EA = {s: sb(es, "EA%d" % s, [128, nchs[s], 2, 4]) for s in (0, 1)}
                EK = {s: sb(es, "EK%d" % s, [128, nchs[s], 2, 4]) for s in (0, 1)}
                EFL = {s: sb(es, "EFL%d" % s, [128, nchs[s], 2, 4]) for s in (0, 1)}
                ECL = {s: sb(es, "ECL%d" % s, [128, nchs[s], 2, 4]) for s in (0, 1)}
                negln16 = sb(es, "negln16", [128, 1])
                onec = sb(es, "onec", [128, 1])
                pool(lambda e: e.memset(negln16[:], -LN16), [], [negln16])
                pool(lambda e: e.memset(onec[:], 1.0), [], [onec])
                for s in (1, 0):
                    nch = nchs[s]
                    dma("sp", G[s][:], gts_d[GOFF[s]:GOFF[s] + nch * 128, :].rearrange("(c p) f -> p c f", p=128), [], [G[s]])
                    fgv = fap(G[s][:], [[16, nch], [8, 2], [1, 4]], off=4)
                    igv = fap(G[s][:], [[16, nch], [8, 2], [1, 4]], off=0)
                    act(lambda e, s=s, fgv=fgv: e.activation(out=NLF[s][:], in_=fgv, func=AF.Exp, scale=-1.0), [G[s]], [NLF[s]])
                    act(lambda e, s=s: e.activation(out=NLF[s][:], in_=NLF[s][:], func=AF.Ln, bias=onec[:], scale=1.0),
                        [NLF[s], onec], [NLF[s]])
                    pg = psb[0] if s == 0 else psb[1]
                    for c in range(nch):
                        mm(pg[:, c * 16:c * 16 + 4], tri_f[:], NLF[s][:, c, 0, :], True, True, [tri_f, NLF[s]], [pg])
                        mm(pg[:, c * 16 + 4:c * 16 + 8], tri_b[:], NLF[s][:, c, 1, :], True, True, [tri_b, NLF[s]], [pg])
                        mm(pg[:, c * 16 + 8:c * 16 + 16], ones_f[:], fap(NLF[s][:], [[1, 8]], off=c * 8), True, True, [ones_f, NLF[s]], [pg])
                    ncb = fap(pg[:], [[16, nch], [4, 2], [1, 4]], off=0)
                    ncl = fap(pg[:], [[16, nch], [4, 2], [1, 4]], off=8)
                    dve(lambda e, s=s, igv=igv, ncb=ncb: e.tensor_tensor(out=EA[s][:], in0=ncb, in1=igv, op=ALU.add), [pg, G[s]], [EA[s]])
                    dve(lambda e, s=s, ncl=ncl: e.tensor_tensor(out=EK[s][:], in0=EA[s][:], in1=ncl, op=ALU.subtract), [pg, EA[s]], [EK[s]])
                    act(lambda e, s=s: e.activation(out=EA[s][:], in_=EA[s][:], func=AF.Exp, bias=negln16[:], scale=1.0), [EA[s], negln16], [EA[s]])
                    act(lambda e, s=s: e.activation(out=EK[s][:], in_=EK[s][:], func=AF.Exp, bias=negln16[:], scale=1.0), [EK[s], negln16], [EK[s]])
                    act(lambda e, s=s, ncb=ncb: e.activation(out=EFL[s][:], in_=ncb, func=AF.Exp), [pg], [EFL[s]])
                    act(lambda e, s=s, ncl=ncl: e.activation(out=ECL[s][:], in_=ncl, func=AF.Exp, scale=-1.0), [pg], [ECL[s]])
                P.barrier()
                if dbg and l == 0:
                    for nm, tt in (("EA", EA), ("EK", EK), ("EFL", EFL), ("ECL", ECL), ("NLF", NLF)):
                        for s_ in (0, 1):
                            dd = nc.dram_tensor("dbg_%s%d" % (nm, s_), [128, nchs[s_], 2, 4], F32, kind="ExternalOutput").ap()
                            dma("sp", dd, tt[s_][:], [tt[s_]], [])

                qb = Ring([sb(es, "qb%d" % i, [128, 8, 512], BF16) for i in range(3)])
                kb = Ring([sb(es, "kb%d" % i, [128, 8, 512], BF16) for i in range(3)])
                ktmb = Ring([sb(es, "ktmb%d" % i, [128, 4, 1024], BF16) for i in range(3)])
                vbb = Ring([sb(es, "vbb%d" % i, [128, 4, 1024], BF16) for i in range(3)])
                Wt = Ring([sb(es, "Wt%d" % i, [128, 4, 128]) for i in range(3)])
                Sd = Ring([sb(es, "Sd%d" % i, [128, 512], BF16) for i in range(3)])
                k2 = Ring([sb(es, "k2_%d" % i, [128, 1024], BF16) for i in range(3)])
                hbuf = Ring([sb(es, "hbuf%d" % i, [128, 1024]) for i in range(3)])
                sm = Ring([sb(es, "sm%d" % i, [128, 16]) for i in range(4)])
                C32r = Ring([sb(es, "C32_%d" % i, [128, 4, 2, 256]) for i in range(2)])
                Cbr = Ring([sb(es, "Cb_%d" % i, [128, 4, 2, 256], BF16) for i in range(2)])
                n32r = Ring([sb(es, "n32_%d" % i, [128, 4, 2]) for i in range(2)])
                nbr = Ring([sb(es, "nb_%d" % i, [128, 4, 2], BF16) for i in range(2)])
                pS, pn0, pn1, pD = psb[0], psb[1], psb[2], psb[3]
                pC = psb[4:8]

                def bdir(d):
                    st = {"C32": C32r.next(), "n32": n32r.next(), "Cb": Cbr.next(), "nb": nbr.next()}
                    dve(lambda e, t_=st["C32"]: e.memset(t_[:], 0.0), [], [st["C32"]])
                    dve(lambda e, t_=st["n32"]: e.memset(t_[:], 0.0), [], [st["n32"]])
                    pool(lambda e, t_=st["Cb"]: e.memset(t_[:], 0.0), [], [st["Cb"]])
                    pool(lambda e, t_=st["nb"]: e.memset(t_[:], 0.0), [], [st["nb"]])
                    maskd = tri_f if d == 0 else tri_b
                    hdst = hf_d if d == 0 else hbk_d
                    blks = list(all_blocks)
                    if d == 1:
                        blks = [b_ for b_ in blks if b_[0] == 1][::-1] + [b_ for b_ in blks if b_[0] == 0][::-1]

                    def load_block(bk):
                        s, t0, n = bk
                        g0 = GOFF[s] + t0
                        ncb_ = n // 128
                        q_ = qb.next(); k_ = kb.next(); ktm_ = ktmb.next(); v_ = vbb.next()
                        dma("sp", q_[:, :, 0:n], qcT[:, g0:g0 + n].rearrange("(c p) t -> p c t", p=128), [], [q_])
                        dma("act", k_[:, :, 0:n], kcT[:, g0:g0 + n].rearrange("(c p) t -> p c t", p=128), [], [k_])
                        dma("sp", ktm_[:, 0:ncb_, :], ktm_d[g0:g0 + n, :].rearrange("(c p) f -> p c f", p=128), [], [ktm_])
                        dma("act", v_[:, 0:ncb_, :], vb_d[g0:g0 + n, :].rearrange("(c p) f -> p c f", p=128), [], [v_])
                        return (q_, k_, ktm_, v_)

                    def chunk_gen():
                        bufs = load_block(blks[0])
                        for bi, (s, t0, n) in enumerate(blks):
                            nxt_bufs = None
                            ncb_ = n // 128
                            corder = list(range(ncb_) if d == 0 else range(ncb_ - 1, -1, -1))
                            for ci, c in enumerate(corder):
                                if ci == 0 and bi + 1 < len(blks):
                                    nxt_bufs = load_block(blks[bi + 1])
                                yield dict(s=s, c=c, cg=t0 // 128 + c, tg=GOFF[s] + t0 + c * 128, bufs=bufs)
                            bufs = nxt_bufs

                    def prep(ch):
                        s, c, cg = ch["s"], ch["c"], ch["cg"]
                        q_, k_, ktm_, v_ = ch["bufs"]
                        cs_ = slice(c * 128, (c + 1) * 128)
                        W = Wt.next()
                        pool(lambda e, W=W, s=s, cg=cg: e.tensor_tensor(
                            out=W[:], in0=fap(maskd[:], [[0, 4], [1, 128]]),
                            in1=fap(EA[s][:], [[1, 4], [0, 128]], off=cg * 8 + d * 4), op=ALU.mult), [maskd, EA[s]], [W])
                        kk = k2.next()
                        pool(lambda e, kk=kk, ktm_=ktm_, c=c, s=s, cg=cg: e.tensor_tensor(
                            out=kk[:].rearrange("p (h e) -> p h e", h=4), in0=ktm_[:, c, :].rearrange("p (h e) -> p h e", h=4),
                            in1=fap(EK[s][:], [[1, 4], [0, 256]], off=cg * 8 + d * 4), op=ALU.mult), [ktm_, EK[s]], [kk])
                        for h in range(4):
                            for dh in range(2):
                                mm(pS[:, h * 128:(h + 1) * 128], k_[:, h * 2 + dh, cs_], q_[:, h * 2 + dh, cs_], dh == 0, dh == 1,
                                   [k_, q_], [pS])
                        sd = Sd.next()
                        dve(lambda e, sd=sd, W=W: e.tensor_tensor(out=sd[:], in0=pS[:], in1=W[:].rearrange("p h t -> p (h t)"),
                                                                  op=ALU.mult), [pS, W], [sd])
                        return (kk, sd)

                    def main(ch, pr):
                        s, c, cg, tg = ch["s"], ch["c"], ch["cg"], ch["tg"]
                        q_, k_, ktm_, v_ = ch["bufs"]
                        kk, sd = pr
                        cs_ = slice(c * 128, (c + 1) * 128)
                        for h in range(4):
                            for dh in range(2):
                                mm(pC[h][:, dh * 256:(dh + 1) * 256], kk[:, h * 256 + dh * 128:h * 256 + (dh + 1) * 128],
                                   v_[:, c, h * 256:(h + 1) * 256], True, True, [kk, v_], [pC[h]])
                        for h in range(4):
                            for dh in range(2):
                                mm(pD[:, 8 + h * 2 + dh:9 + h * 2 + dh], kk[:, h * 256 + dh * 128:h * 256 + (dh + 1) * 128],
                                   ones_bf[:, 0:1], True, True, [kk, ones_bf], [pD])
                        Cold, nold, Cbold, nbold = st["C32"], st["n32"], st["Cb"], st["nb"]
                        Cnew, nnew, Cbnew, nbnew = C32r.next(), n32r.next(), Cbr.next(), nbr.next()
                        for h in range(4):
                            dve(lambda e, h=h, Cold=Cold, Cnew=Cnew: e.scalar_tensor_tensor(
                                out=Cnew[:, h, :, :].rearrange("p a b -> p (a b)"), in0=Cold[:, h, :, :].rearrange("p a b -> p (a b)"),
                                scalar=ECL[s][:, cg, d, h:h + 1], in1=pC[h][:, :], op0=ALU.mult, op1=ALU.add),
                                [Cold, ECL[s], pC[h]], [Cnew])
                        pool(lambda e, nold=nold, nnew=nnew: e.tensor_tensor(
                            out=nnew[:], in0=nold[:], in1=fap(ECL[s][:], [[1, 4], [0, 2]], off=cg * 8 + d * 4), op=ALU.mult),
                            [nold, ECL[s]], [nnew])
                        dve(lambda e, nnew=nnew: e.tensor_tensor(out=nnew[:], in0=nnew[:], in1=fap(pD[:], [[2, 4], [1, 2]], off=8), op=ALU.add),
                            [nnew, pD], [nnew])
                        act(lambda e, Cbnew=Cbnew, Cnew=Cnew: e.copy(out=Cbnew[:], in_=Cnew[:]), [Cnew], [Cbnew])
                        pool(lambda e, nbnew=nbnew, nnew=nnew: e.tensor_copy(out=nbnew[:], in_=nnew[:]), [nnew], [nbnew])
                        st["C32"], st["n32"], st["Cb"], st["nb"] = Cnew, nnew, Cbnew, nbnew
                        for h in range(4):
                            pn = pn0 if h < 2 else pn1
                            reg = pn[:, (h % 2) * 256:(h % 2 + 1) * 256]
                            mm(reg, sd[:, h * 128:(h + 1) * 128], v_[:, c, h * 256:(h + 1) * 256], True, False, [sd, v_], [pn])
                            mm(reg, q_[:, h * 2, cs_], Cbold[:, h, 0, :], False, False, [q_, Cbold], [pn])
                            mm(reg, q_[:, h * 2 + 1, cs_], Cbold[:, h, 1, :], False, True, [q_, Cbold], [pn])
                        for h in range(4):
                            mm(pD[:, h:h + 1], sd[:, h * 128:(h + 1) * 128], ones_bf[:, 0:1], True, False, [sd, ones_bf], [pD])
                            mm(pD[:, h:h + 1], q_[:, h * 2, cs_], nbold[:, h, 0:1], False, False, [q_, nbold], [pD])
                            mm(pD[:, h:h + 1], q_[:, h * 2 + 1, cs_], nbold[:, h, 1:2], False, True, [q_, nbold], [pD])
                        dn = sm.next()
                        dve(lambda e, dn=dn: e.tensor_tensor(out=dn[:, 8:12], in0=pD[:, 0:4], in1=EFL[s][:, cg, d, :], op=ALU.max),
                            [pD, EFL[s]], [dn])
                        dve(lambda e, dn=dn: e.scalar_tensor_tensor(out=dn[:, 0:4], in0=pD[:, 0:4], scalar=-1.0, in1=dn[:, 8:12],
                                                                    op0=ALU.mult, op1=ALU.max), [pD, dn], [dn])
                        dve(lambda e, dn=dn: e.reciprocal(out=dn[:, 4:8], in_=dn[:, 0:4]), [dn], [dn])
                        if not (last and s == 1):
                            hb = hbuf.next()
                            for h in range(4):
                                pn = pn0 if h < 2 else pn1
                                act(lambda e, hb=hb, pn=pn, h=h, dn=dn: e.activation(
                                    out=hb[:, h * 256:(h + 1) * 256], in_=pn[:, (h % 2) * 256:(h % 2 + 1) * 256], func=AF.Identity,
                                    scale=dn[:, 4 + h:5 + h]), [pn, dn], [hb])
                            dma("sp", hdst[tg:tg + 128, :], hb[:], [hb], [])

                    gen = chunk_gen()
                    cur = next(gen)
                    cur_p = prep(cur)
                    while cur is not None:
                        nxt = next(gen, None)
                        nxt_p = prep(nxt) if nxt is not None else None
                        main(cur, cur_p)
                        cur, cur_p = nxt, nxt_p
                    P.barrier()
                bdir(0)
                bdir(1)

            with ExitStack() as es:
                hfin = Ring([sb(es, "hfin%d" % i, [128, 1024]) for i in range(3)])
                hbin = Ring([sb(es, "hbin%d" % i, [128, 1024]) for i in range(3)])
                hn = Ring([sb(es, "hn%d" % i, [128, 1024]) for i in range(2)])
                gbb = Ring([sb(es, "gbb%d" % i, [128, 8, 512], BF16) for i in range(3)])
                ybb = Ring([sb(es, "ybb%d" % i, [128, 8, 512], BF16) for i in range(3)])
                tiles3 = []
                junk = sb(es, "junk", [128, 256])
                sm3 = Ring([sb(es, "sm3_%d" % i, [128, 16]) for i in range(4)])
                ptr = Ring(psb)
                for (s, t0, n) in all_blocks:
                    if last and s == 1:
                        continue
                    g0 = GOFF[s] + t0
                    for c in range(n // 128):
                        tiles3.append((g0 + c * 128, slice(c * 128, (c + 1) * 128), c == 0, c == n // 128 - 1, g0, n))
                blkbufs = {}

                def b3_stage_a(tl):
                    tg, _, is_first, _, g0, n = tl
                    if is_first:
                        gb_ = gbb.next(); yb_ = ybb.next()
                        dma("act", gb_[:, :, 0:n], gBT[:, g0:g0 + n].rearrange("(c p) t -> p c t", p=128), [], [gb_])
                        blkbufs[g0] = (gb_, yb_)
                    hfi = hfin.next(); hbi = hbin.next()
                    dma("sp", hfi[:], hf_d[tg:tg + 128, :], [], [hfi])
                    dma("sp", hbi[:], hbk_d[tg:tg + 128, :], [], [hbi])
                    pool(lambda e, hfi=hfi, hbi=hbi: e.tensor_tensor(out=hfi[:], in0=hfi[:], in1=hbi[:], op=ALU.add), [hfi, hbi], [hfi])
                    st_ = sm3.next()
                    for h in range(4):
                        act(lambda e, hfi=hfi, h=h, st_=st_: e.activation(
                            out=junk[:], in_=hfi[:, h * 256:(h + 1) * 256], func=AF.Square, accum_out=st_[:, h:h + 1]),
                            [hfi], [junk, st_])
                    act(lambda e, st_=st_: e.activation(out=st_[:, 0:4], in_=st_[:, 0:4], func=AF.Sqrt, bias=epsc[:], scale=1.0 / 256),
                        [st_, epsc], [st_])
                    dve(lambda e, st_=st_: e.reciprocal(out=st_[:, 4:8], in_=st_[:, 0:4]), [st_], [st_])
                    return hfi, st_

                def b3_stage_b(tl, sa):
                    tg, cs_, _, is_last, g0, n = tl
                    gb_, yb_ = blkbufs[g0]
                    hfi, st_ = sa
                    hn_ = hn.next()
                    for h in range(4):
                        act(lambda e, hn_=hn_, hfi=hfi, st_=st_, h=h: e.activation(
                            out=hn_[:, h * 256:(h + 1) * 256], in_=hfi[:, h * 256:(h + 1) * 256], func=AF.Identity,
                            scale=st_[:, 4 + h:5 + h]), [hfi, st_], [hn_])
                    for half in range(2):
                        pt = ptr.next()
                        for c4 in range(4):
                            chn = half * 4 + c4
                            pe(lambda e, pt=pt, hn_=hn_, chn=chn, c4=c4: e.transpose(
                                out=pt[:, c4 * 128:(c4 + 1) * 128], in_=hn_[:, chn * 128:(chn + 1) * 128], identity=ident[:]),
                                [hn_, ident], [pt])
                        for c4 in range(4):
                            chn = half * 4 + c4
                            dve(lambda e, pt=pt, chn=chn, c4=c4, yb_=yb_, gb_=gb_, cs_=cs_: e.scalar_tensor_tensor(
                                out=yb_[:, chn, cs_], in0=pt[:, c4 * 128:(c4 + 1) * 128], scalar=mnT[:, l, chn:chn + 1],
                                in1=gb_[:, chn, cs_], op0=ALU.mult, op1=ALU.mult), [pt, mnT, gb_], [yb_])
                    if is_last:
                        dma("sp", ymT[512:1536, g0:g0 + n].rearrange("(c p) t -> p c t", p=128), yb_[:, :, 0:n], [yb_], [])

                if tiles3:
                    sa_cur = b3_stage_a(tiles3[0])
                    for i3, tl in enumerate(tiles3):
                        sa_nxt = b3_stage_a(tiles3[i3 + 1]) if i3 + 1 < len(tiles3) else None
                        b3_stage_b(tl, sa_cur)
                        sa_cur = sa_nxt
                P.barrier()
            if stop_after == "B":
                return True

            with ExitStack() as co:
                ycT = sb(co, "ycT", [128, 4, TT], BF16)
                with ExitStack() as es:
                    CH = 32
                    m32f = sb(es, "m32f", [32, 32])
                    m32b = sb(es, "m32b", [32, 32])
                    identc_bf = sb(es, "identc_bf", [128, 128], BF16)
                    pool(lambda e: e.tensor_copy(out=identc_bf[:], in_=ident[:]), [ident], [identc_bf])
                    pool(lambda e: e.tensor_copy(out=m32f[:], in_=tri_f[0:32, 0:32]), [tri_f], [m32f])
                    pool(lambda e: e.tensor_copy(out=m32b[:], in_=tri_b[0:32, 0:32]), [tri_b], [m32b])
                    lfb = Ring([sb(es, "lfb%d" % i, [128, 4, 512]) for i in range(2)])
                    qqb = Ring([sb(es, "qqb%d" % i, [128, 4, 512], BF16) for i in range(2)])
                    kkb = Ring([sb(es, "kkb%d" % i, [128, 4, 512], BF16) for i in range(2)])
                    vvb = Ring([sb(es, "vvb%d" % i, [32, 8, 512], BF16) for i in range(4)])
                    Pc = sb(es, "Pc", [128, 4, 512])
                    Pr = sb(es, "Pr", [128, 4, 512])
                    e1 = sb(es, "e1", [128, 4, 512])
                    e2 = sb(es, "e2", [128, 4, 512])
                    PBs = [dict(qt=sb(es, "qt%d" % i, [128, 4, 512], BF16), ktl=sb(es, "ktl%d" % i, [128, 4, 512], BF16),
                                qh=sb(es, "qh%d" % i, [128, 4, 512], BF16), kh=sb(es, "kh%d" % i, [128, 4, 512], BF16),
                                ecl=sb(es, "ecl%d" % i, [128, 4, 16])) for i in range(2)]
                    smr = sb(es, "smr", [128, 4, 16])
                    smc = sb(es, "smc", [128, 4, 16])
                    smx = sb(es, "smx", [128, 4, 16])
                    oblk = Ring([sb(es, "oblk%d" % i, [128, 4, 512]) for i in range(2)])
                    ofin = e2
                    gcb = sb(es, "gcb", [128, 4, 512], BF16)
                    sqb = sb(es, "sqb", [128, 4, 512], BF16)
                    rsb = e1
                    scT = Ring([sb(es, "scT%d" % i, [32, 128], BF16) for i in range(4)])
                    ktok = Ring([sb(es, "ktok%d" % i, [32, 512], BF16) for i in range(4)])
                    S32r = Ring([sb(es, "S32_%d" % i, [128, 4, 128]) for i in range(2)])
                    Sbr = Ring([sb(es, "Sb_%d" % i, [128, 4, 128], BF16) for i in range(3)])
                    pSC2, pK2, pU2 = psb[0:2], psb[2:4], psb[4:6]
                    pO1, pR1 = psb[6], psb[7]
                    cst = [0, 0, 0]

                    def cdir(d):
                        cstate = {"S32": S32r.next(), "Sb": Sbr.next()}
                        dve(lambda e, t_=cstate["S32"]: e.memset(t_[:], 0.0), [], [cstate["S32"]])
                        pool(lambda e, t_=cstate["Sb"]: e.memset(t_[:], 0.0), [], [cstate["Sb"]])
                        m32 = m32f if d == 0 else m32b
                        blks = list(all_blocks)
                        if d == 1:
                            blks = [b_ for b_ in blks if b_[0] == 1][::-1] + [b_ for b_ in blks if b_[0] == 0][::-1]

                        def make_prep(bk, PB):
                            s, t0, n = bk
                            g0 = GOFF[s] + t0
                            nj = n // CH
                            qt, ktl, qh, kh, ecl = PB["qt"], PB["ktl"], PB["qh"], PB["kh"], PB["ecl"]
                            lf_ = lfb.next(); qq_ = qqb.next(); kk_ = kkb.next()
                            vvs = [vvb.next() for _ in range((nj + 7) // 8)]
                            PB["vv"] = vvs
                            th = []

                            def loads():
                                dma("sp", lf_[:, :, 0:n], lfT[d][:, g0:g0 + n].rearrange("(h p) t -> p h t", p=128), [], [lf_])
                                dma("act", qq_[:, :, 0:n], cqT[:, g0:g0 + n].rearrange("(h p) t -> p h t", p=128), [], [qq_])
                                dma("act", kk_[:, :, 0:n], ckT[d][:, g0:g0 + n].rearrange("(h p) t -> p h t", p=128), [], [kk_])
                                for hi_, vv_ in enumerate(vvs):
                                    nn = min(8, nj - hi_ * 8)
                                    dma("sp", vv_[:, 0:nn, :], ci_d[g0 + hi_ * 256:g0 + hi_ * 256 + nn * CH, :].rearrange("(j i) f -> i j f", i=CH),
                                        [], [vv_])
                            th.append(loads)
                            for h in range(4):
                                th.append(lambda h=h: dve(lambda e: e.tensor_tensor_scan(
                                    out=Pc[:, h, 0:n], data0=fap(ones_f[:], [[0, n]]), data1=lf_[:, h, 0:n], initial=0.0, op0=ALU.mult, op1=ALU.add),
                                    [lf_, ones_f], [Pc]))
                            th.append(lambda: pool(lambda e: e.tensor_copy(out=Pr[:, :, 0:CH], in_=Pc[:, :, 0:CH]), [Pc], [Pr]))
                            if nj > 1:
                                for h in range(4):
                                    th.append(lambda h=h: pool(lambda e: e.tensor_tensor(
                                        out=fap(Pr[:], [[CH, nj - 1], [1, CH]], off=h * 512 + CH),
                                        in0=fap(Pc[:], [[CH, nj - 1], [1, CH]], off=h * 512 + CH),
                                        in1=fap(Pc[:], [[CH, nj - 1], [0, CH]], off=h * 512 + CH - 1), op=ALU.subtract), [Pc], [Pr]))
                            th.append(lambda: dve(lambda e: e.tensor_copy(out=smc[:, :, 0:nj], in_=fap(Pr[:], [[512, 4], [CH, nj]], off=CH - 1)), [Pr], [smc]))
                            if d == 1:
                                th.append(lambda: dve(lambda e: e.tensor_tensor(out=Pr[:, :, 0:n], in0=Pr[:, :, 0:n], in1=lf_[:, :, 0:n], op=ALU.subtract),
                                                      [Pr, lf_], [Pr]))
                            mid = 15 if d == 0 else 16
                            th.append(lambda: dve(lambda e: e.tensor_copy(out=smr[:, :, 0:nj], in_=fap(Pr[:], [[512, 4], [CH, nj]], off=mid)), [Pr], [smr]))
                            for h in range(4):
                                th.append(lambda h=h: pool(lambda e: e.tensor_tensor(
                                    out=fap(Pc[:], [[CH, nj], [1, CH]], off=h * 512), in0=fap(Pr[:], [[CH, nj], [1, CH]], off=h * 512),
                                    in1=fap(smr[:], [[1, nj], [0, CH]], off=h * 16), op=ALU.subtract), [Pr, smr], [Pc]))
                            th.append(lambda: act(lambda e: e.activation(out=e1[:, :, 0:n], in_=Pc[:, :, 0:n], func=AF.Exp), [Pc], [e1]))
                            th.append(lambda: act(lambda e: e.activation(out=e2[:, :, 0:n], in_=Pc[:, :, 0:n], func=AF.Exp, scale=-1.0), [Pc], [e2]))
                            eq, ek_ = (e1, e2) if d == 0 else (e2, e1)
                            th.append(lambda: pool(lambda e: e.tensor_tensor(out=qt[:, :, 0:n], in0=qq_[:, :, 0:n], in1=eq[:, :, 0:n], op=ALU.mult),
                                                   [qq_, eq], [qt]))
                            th.append(lambda: pool(lambda e: e.tensor_tensor(out=ktl[:, :, 0:n], in0=kk_[:, :, 0:n], in1=ek_[:, :, 0:n], op=ALU.mult),
                                                   [kk_, ek_], [ktl]))
                            th.append(lambda: dve(lambda e: e.tensor_tensor(out=smx[:, :, 0:nj], in0=smc[:, :, 0:nj], in1=smr[:, :, 0:nj], op=ALU.subtract),
                                                  [smc, smr], [smx]))
                            th.append(lambda: act(lambda e: e.activation(out=ecl[:, :, 0:nj], in_=smc[:, :, 0:nj], func=AF.Exp), [smc], [ecl]))
                            th.append(lambda: act(lambda e: e.activation(out=smr[:, :, 0:nj], in_=smr[:, :, 0:nj], func=AF.Exp), [smr], [smr]))
                            th.append(lambda: act(lambda e: e.activation(out=smx[:, :, 0:nj], in_=smx[:, :, 0:nj], func=AF.Exp), [smx], [smx]))
                            fq, fk = (smr, smx) if d == 0 else (smx, smr)
                            for h in range(4):
                                th.append(lambda h=h: pool(lambda e: e.tensor_tensor(
                                    out=fap(qh[:], [[CH, nj], [1, CH]], off=h * 512), in0=fap(qt[:], [[CH, nj], [1, CH]], off=h * 512),
                                    in1=fap(fq[:], [[1, nj], [0, CH]], off=h * 16), op=ALU.mult), [qt, fq], [qh]))
                                th.append(lambda h=h: pool(lambda e: e.tensor_tensor(
                                    out=fap(kh[:], [[CH, nj], [1, CH]], off=h * 512), in0=fap(ktl[:], [[CH, nj], [1, CH]], off=h * 512),
                                    in1=fap(fk[:], [[1, nj], [0, CH]], off=h * 16), op=ALU.mult), [ktl, fk], [kh]))
                            return th

                        def make_readout(bk, ob):
                            s, t0, n = bk
                            g0 = GOFF[s] + t0
                            th = []
                            if d == 0:
                                if not (last and s == 1):
                                    th.append(lambda: dma("sp", ofT[:, g0:g0 + n].rearrange("(h p) t -> p h t", p=128), ob[:, :, 0:n], [ob], []))
                                return th
                            if last and s == 1:
                                return th

                            def loads():
                                dma("sp", ofin[:, :, 0:n], ofT[:, g0:g0 + n].rearrange("(h p) t -> p h t", p=128), [], [ofin])
                                dma("act", gcb[:, :, 0:n], gCT[:, g0:g0 + n].rearrange("(h p) t -> p h t", p=128), [], [gcb])
                            th.append(loads)
                            th.append(lambda: pool(lambda e: e.tensor_tensor(out=ob[:, :, 0:n], in0=ob[:, :, 0:n], in1=ofin[:, :, 0:n], op=ALU.add),
                                                   [ob, ofin], [ob]))
                            th.append(lambda: act(lambda e: e.activation(out=sqb[:, :, 0:n], in_=ob[:, :, 0:n], func=AF.Square), [ob], [sqb]))
                            for tq in range(n // 128):
                                def grp(tq=tq):
                                    for h in range(4):
                                        mm(pR1[:, h * 128:(h + 1) * 128], ones_bf[:], sqb[:, h, tq * 128:(tq + 1) * 128], True, True, [ones_bf, sqb], [pR1])
                                    act(lambda e: e.activation(out=rsb[:, :, tq * 128:(tq + 1) * 128],
                                                               in_=pR1[:, 0:512].rearrange("p (h t) -> p h t", h=4), func=AF.Ln, bias=epsc[:],
                                                               scale=1.0 / 128), [pR1, epsc], [rsb])
                                th.append(grp)
                            th.append(lambda: act(lambda e: e.activation(out=rsb[:, :, 0:n], in_=rsb[:, :, 0:n], func=AF.Exp, scale=-0.5), [rsb], [rsb]))
                            for h in range(4):
                                th.append(lambda h=h: dve(lambda e: e.scalar_tensor_tensor(
                                    out=rsb[:, h, 0:n], in0=ob[:, h, 0:n], scalar=hnT[:, l, h:h + 1], in1=rsb[:, h, 0:n],
                                    op0=ALU.mult, op1=ALU.mult), [ob, hnT, rsb], [rsb]))
                                if s == 1:
                                    o_ap = ycT[:, h, t0:t0 + n]
                                    i0 = rsb[:, h, 0:n]
                                    i1 = gcb[:, h, 0:n]
                                else:
                                    ncb_ = n // ROWS
                                    c0 = t0 // ROWS
                                    o_ap = ycT[:, h, TC:TT].rearrange("p (r c) -> p c r", c=64)[:, c0:c0 + ncb_, :]
                                    i0 = rsb[:, h, 0:n].rearrange("p (c r) -> p c r", r=ROWS)
                                    i1 = gcb[:, h, 0:n].rearrange("p (c r) -> p c r", r=ROWS)
                                th.append(lambda h=h: pool(lambda e: e.tensor_tensor(out=rsb[:, h, 0:n], in0=rsb[:, h, 0:n], in1=gcb[:, h, 0:n], op=ALU.mult),
                                                           [rsb, gcb], [rsb]))
                                th.append(lambda o_ap=o_ap, i0=i0: act(lambda e: e.copy(out=o_ap, in_=i0), [rsb], [ycT]))
                            return th

                        def run_chunks(bk, PB, ob, side):
                            s, t0, n = bk
                            nj = n // CH
                            qt, ktl, qh, kh, ecl, vvs = PB["qt"], PB["ktl"], PB["qh"], PB["kh"], PB["ecl"], PB["vv"]
                            jlist = list(range(nj) if d == 0 else range(nj - 1, -1, -1))
                            per = (len(side) + nj - 1) // nj if side else 0

                            def vv_of(j):
                                return vvs[j // 8], j % 8

                            def stage1(j):
                                js = slice(j * CH, (j + 1) * CH)
                                pSCj = pSC2[cst[0] % 2]
                                pKj = pK2[cst[0] % 2]
                                cst[0] += 1
                                for h in range(4):
                                    mm(pSCj[0:32, h * 32:(h + 1) * 32], ktl[:, h, js], qt[:, h, js], True, True, [ktl, qt], [pSCj])
                                sc_ = scT.next()
                                dve(lambda e, sc_=sc_, pSCj=pSCj: e.tensor_tensor(out=sc_[:].rearrange("p (h t) -> p h t", h=4),
                                                                        in0=pSCj[0:32, 0:128].rearrange("p (h t) -> p h t", h=4),
                                                                        in1=fap(m32[:], [[0, 4], [1, 32]]), op=ALU.mult), [pSCj, m32], [sc_])
                                pkv = pKj[:, :].bitcast(BF16)
                                for h in range(4):
                                    pe(lambda e, h=h, js=js, pkv=pkv: e.transpose(out=pkv[0:32, h * 128:(h + 1) * 128], in_=kh[:, h, js],
                                                                                     identity=identc_bf[:]), [kh, identc_bf], [pKj])
                                kt_ = ktok.next()
                                act(lambda e, kt_=kt_, pkv=pkv: e.copy(out=kt_[:], in_=pkv[0:32, 0:512]), [pKj], [kt_])
                                return sc_, kt_

                            def stage_u(j, kt_):
                                vv_, jj = vv_of(j)
                                pUj = pU2[cst[1] % 2]
                                cst[1] += 1
                                for h in range(4):
                                    mm(pUj[:, h * 128:(h + 1) * 128], kt_[0:32, h * 128:(h + 1) * 128], vv_[0:32, jj, h * 128:(h + 1) * 128], True, True,
                                       [kt_, vv_], [pUj])
                                Sold = cstate["S32"]
                                Snew, Sbnew = S32r.next(), Sbr.next()
                                dve(lambda e, j=j, Sold=Sold, Snew=Snew: e.tensor_tensor(
                                    out=Snew[:], in0=Sold[:], in1=fap(ecl[:], [[16, 4], [0, 128]], off=j), op=ALU.mult), [Sold, ecl], [Snew])
                                dve(lambda e, Snew=Snew, pUj=pUj: e.tensor_tensor(
                                    out=Snew[:].rearrange("p h e -> p (h e)"), in0=Snew[:].rearrange("p h e -> p (h e)"), in1=pUj[:, 0:512],
                                    op=ALU.add), [Snew, pUj], [Snew])
                                act(lambda e, Sbnew=Sbnew, Snew=Snew: e.copy(out=Sbnew[:], in_=Snew[:]), [Snew], [Sbnew])
                                sb_before = cstate["Sb"]
                                cstate["S32"], cstate["Sb"] = Snew, Sbnew
                                return sb_before

                            def stage_o(j, sc_, Sbold):
                                vv_, jj = vv_of(j)
                                js = slice(j * CH, (j + 1) * CH)
                                for h in range(4):
                                    reg = pO1[:, h * 32:(h + 1) * 32]
                                    mm(reg, vv_[0:32, jj, h * 128:(h + 1) * 128], sc_[0:32, h * 32:(h + 1) * 32], True, False, [vv_, sc_], [pO1])
                                    mm(reg, Sbold[:, h, :], qh[:, h, js], False, True, [Sbold, qh], [pO1])
                                act(lambda e, js=js: e.copy(out=ob[:, :, js], in_=pO1[:, 0:128].rearrange("p (h t) -> p h t", h=4)),
                                    [pO1], [ob])

                            nj_ = len(jlist)
                            s1 = {}
                            sbin = {}
                            for i in range(min(2, nj_)):
                                s1[i] = stage1(jlist[i])
                            sbin[0] = stage_u(jlist[0], s1[0][1])
                            si = 0
                            for i in range(nj_):
                                if i + 2 < nj_:
                                    s1[i + 2] = stage1(jlist[i + 2])
                                if i + 1 < nj_:
                                    sbin[i + 1] = stage_u(jlist[i + 1], s1[i + 1][1])
                                stage_o(jlist[i], s1[i][0], sbin[i])
                                for _ in range(per):
                                    if si < len(side):
                                        side[si]()
                                        si += 1
                            while si < len(side):
                                side[si]()
                                si += 1

                        for t_ in make_prep(blks[0], PBs[0]):
                            t_()
                        pending = []
                        for bi, bk in enumerate(blks):
                            side = list(pending)
                            if bi + 1 < len(blks):
                                side += make_prep(blks[bi + 1], PBs[(bi + 1) % 2])
                            ob = oblk.next()
                            run_chunks(bk, PBs[bi % 2], ob, side)
                            pending = make_readout(bk, ob)
                        for t_ in pending:
                            t_()
                        P.barrier()
                    cdir(0)
                    cdir(1)
                if dbg:
                    dbg_yc = nc.dram_tensor("dbg_yc%d" % l, [128, 4, TT], BF16, kind="ExternalOutput").ap()
                    dma("sp", dbg_yc, ycT[:], [ycT], [])
                if stop_after == "C":
                    return True

                with ExitStack() as es:
                    wo = sb(es, "wo", [128, KD, D], BF16)
                    ymb = Ring([sb(es, "ymb%d" % i, [128, 12, 512], BF16) for i in range(2)])
                    hjb = Ring([sb(es, "hjb%d" % i, [128, 4, 512]) for i in range(2)])
                    hob = Ring([sb(es, "hob%d" % i, [128, 4, 512]) for i in range(2)])
                    pring = Ring(psb)
                    for c in range(8):
                        dma("pool", wo[:, :, c * 256:(c + 1) * 256], w_out[l, :, c * 256:(c + 1) * 256].rearrange("(k p) f -> p k f", p=128), [], [wo])
                    for (s, t0, n) in all_blocks:
                        if last and s == 1:
                            continue
                        g0 = GOFF[s] + t0
                        ym = ymb.next()
                        dma("sp", ym[:, :, 0:n], ymT[:, g0:g0 + n].rearrange("(k p) t -> p k t", p=128), [], [ym])
                        for jg in range(4):
                            hj = hjb.next(); ho = hob.next()
                            dma("act", hj[:, :, 0:n], src_h[s][jg * 512:(jg + 1) * 512, t0:t0 + n].rearrange("(j p) t -> p j t", p=128), [], [hj])
                            for jj in range(4):
                                j = jg * 4 + jj
                                pb = pring.next()
                                for k in range(KD):
                                    rhs = ym[:, k, 0:n] if k < 12 else ycT[:, k - 12, g0:g0 + n]
                                    mm(pb[:, 0:n], wo[:, k, j * 128:(j + 1) * 128], rhs, k == 0, k == KD - 1, [wo, ym, ycT], [pb])
                                dve(lambda e, pb=pb, j=j, jj=jj, hj=hj, ho=ho, n=n, s=s: e.scalar_tensor_tensor(
                                    out=ho[:, jj, 0:n], in0=pb[:, 0:n], scalar=modT[:, l, s, 32 + j:33 + j], in1=hj[:, jj, 0:n],
                                    op0=ALU.mult, op1=ALU.add), [pb, modT, hj], [ho])
                            dma("sp", hT[s][jg * 512:(jg + 1) * 512, t0:t0 + n].rearrange("(j p) t -> p j t", p=128), ho[:, :, 0:n], [ho], [])
                    P.barrier()
            return False

        def final_norm():
            with ExitStack() as es:
                hin = [sb(es, "fhin%d" % i, [128, KD, 512]) for i in range(2)]
                sq = sb(es, "fsq", [128, KD, 512], BF16)
                rstd = sb(es, "frstd", [128, 512])
                ot = [sb(es, "fot%d" % i, [128, KD, 512]) for i in range(2)]
                for bi, (t0, n) in enumerate(blocks(0)):
                    h = hin[bi % 2]
                    o = ot[bi % 2]
                    dma("sp", h[:, :, 0:n], hT[0][:, t0:t0 + n].rearrange("(k p) t -> p k t", p=128), [], [h])
                    act(lambda e, h=h, n=n: e.activation(out=sq[:, :, 0:n], in_=h[:, :, 0:n], func=AF.Square), [h], [sq])
                    pss = psb[bi % 2]
                    for k in range(KD):
                        mm(pss[:, 0:n], ones_bf[:], sq[:, k, 0:n], k == 0, k == KD - 1, [ones_bf, sq], [pss])
                    act(lambda e, pss=pss, n=n: e.activation(out=rstd[:, 0:n], in_=pss[:, 0:n], func=AF.Sqrt,
                                                             bias=epsc[:], scale=1.0 / D), [pss, epsc], [rstd])
                    dve(lambda e, n=n: e.reciprocal(out=rstd[:, 0:n], in_=rstd[:, 0:n]), [rstd], [rstd])
                    for k in range(KD):
                        dve(lambda e, h=h, o=o, k=k, n=n: e.scalar_tensor_tensor(
                            out=o[:, k, 0:n], in0=h[:, k, 0:n], scalar=fnT[:, k:k + 1], in1=rstd[:, 0:n],
                            op0=ALU.mult, op1=ALU.mult), [h, fnT, rstd], [o])
                    dma("act", outT[:, t0:t0 + n].rearrange("(k p) t -> p k t", p=128), o[:, :, 0:n], [o], [])
                P.barrier()

        stopped = False
        for l in range(DEPTH):
            if layer(l):
                stopped = True
                break
        if not stopped:
            final_norm()
        P.barrier()
        P.emit()
    return nc


def host_layout(inp, b, DEPTH):
    f = lambda a: np.ascontiguousarray(a, dtype=np.float32)
    d = {}
    d["xT"] = f(inp["x"][b].T)
    d["cxT"] = f(inp["ctx"][b].T)
    cs = np.stack([inp["c"][b].reshape(KD, 128).T, inp["c_ctx"].reshape(KD, 128).T], axis=-1)
    d["cs"] = f(cs)
    d["w_ada"] = f(inp["w_ada"])
    d["badaT"] = f(inp["b_ada"].reshape(DEPTH, 48, 128).transpose(2, 0, 1))
    d["normgT"] = f(inp["norm_g"].reshape(DEPTH, KD, 128).transpose(2, 0, 1))
    d["w_in"] = f(inp["w_in"])
    d["b_in"] = f(inp["b_in"])
    idx = np.array(FM_STARTS)[None, :] + np.arange(128)[:, None]
    d["bfm"] = f(inp["b_in"][:, idx].transpose(1, 0, 2))
    d["wsT"] = f(inp["w_spatial"].transpose(0, 3, 1, 2))
    d["bs"] = f(inp["b_spatial"].reshape(DEPTH, 512))
    d["convT"] = f(inp["conv_qk"].reshape(DEPTH, 3, 16, 128).transpose(3, 0, 2, 1))
    d["mnT"] = f(inp["mlstm_norm"].reshape(DEPTH, 8, 128).transpose(2, 0, 1))
    d["hnT"] = f(inp["hgrn_norm"].reshape(DEPTH, 4, 128).transpose(2, 0, 1))
    d["lbT"] = f(inp["hgrn_lb_logits"].reshape(DEPTH, 4, 128).transpose(2, 1, 0))
    d["fnT"] = f(inp["final_norm"].reshape(KD, 128).T)
    d["w_out"] = f(inp["w_out"])
    return d


_NC_CACHE = {}


def kernel(**inputs):
    inp = {k: np.asarray(v) for k, v in inputs.items()}
    B, TL, _ = inp["x"].shape
    TC = inp["ctx"].shape[1]
    DEPTH = inp["w_in"].shape[0]
    key = (TL, TC, DEPTH)
    if key not in _NC_CACHE:
        _NC_CACHE[key] = build(TL, TC, DEPTH)
    nc = _NC_CACHE[key]
    in_maps = [host_layout(inp, b, DEPTH) for b in range(B)]
    res = run_bass_kernel_spmd(nc, in_maps, core_ids=list(range(B)))
    out = np.stack([np.ascontiguousarray(r["outT"].T) for r in res.results], axis=0)
    return out.astype(np.float32)
```

```python
import numpy as np
from contextlib import ExitStack
import concourse.bass as bass
import concourse.mybir as mybir
from concourse.bass_utils import run_bass_kernel_spmd

F32 = mybir.dt.float32
BF16 = mybir.dt.bfloat16
AF = mybir.ActivationFunctionType
ALU = mybir.AluOpType
AX = mybir.AxisListType

ENGS = ("pe", "act", "dve", "pool", "sp")
N_DSEM = 24

D = 2048
KD = 16
DIN = 9232
EPS = 1e-6
C_AU, C_AV, C_AZ = 0, 512, 1024
C_BQ, C_BK, C_BV, C_BO, C_BZ = 1536, 2560, 3584, 4608, 5632
C_GT = 6656
C_CQ, C_CFF, C_CFB, C_CI, C_CG = 6672, 7184, 7696, 8208, 8720

FM_STARTS = ([C_AU + i * 128 for i in range(4)] + [C_AZ + i * 128 for i in range(4)]
             + [C_BQ + i * 128 for i in range(8)] + [C_BK + i * 128 for i in range(8)]
             + [C_BO + i * 128 for i in range(8)] + [C_BZ + i * 128 for i in range(8)]
             + [C_CQ + i * 128 for i in range(4)] + [C_CFF + i * 128 for i in range(4)]
             + [C_CFB + i * 128 for i in range(4)] + [C_CG + i * 128 for i in range(4)])
FM_IDX = {c: i for i, c in enumerate(FM_STARTS)}
NFM = len(FM_STARTS)


class T:
    __slots__ = ("name", "lw", "rd")

    def __init__(self, name=""):
        self.name = name
        self.lw = None
        self.rd = []


class Prog:
    def __init__(self, nc):
        self.nc = nc
        self.ops = {e: [] for e in ENGS}
        self.known = {e: {o: -1 for o in ENGS} for e in ENGS}
        self.known_d = {e: [0] * N_DSEM for e in ENGS}
        self.dmas = []

    def _deps(self, eng, reads, writes):
        deps = []
        for t in reads:
            if t.lw is not None:
                deps.append(t.lw)
        for t in writes:
            if t.lw is not None:
                deps.append(t.lw)
            deps.extend(t.rd)
        best = {}
        dd = {}
        for d in deps:
            if d[0] == "d":
                slot, val = self.dmas[d[1]]
                if self.known_d[eng][slot] < val:
                    dd[slot] = max(dd.get(slot, 0), val)
            else:
                _, oe, idx = d
                if oe == eng and eng in ("pe", "sp"):
                    continue
                if self.known[eng][oe] < idx:
                    best[oe] = max(best.get(oe, -1), idx)
        waits = []
        for oe, idx in best.items():
            self.known[eng][oe] = idx
            self.ops[oe][idx][2] = True
            waits.append(("e", oe, idx))
        for slot, val in dd.items():
            self.known_d[eng][slot] = val
            waits.append(("d", slot, val))
        return waits

    def op(self, eng, fn, reads=(), writes=()):
        waits = self._deps(eng, reads, writes)
        idx = len(self.ops[eng])
        self.ops[eng].append([fn, waits, False, None])
        me = ("e", eng, idx)
        for t in reads:
            t.rd.append(me)
        for t in writes:
            t.lw = me
            t.rd = []
        return me

    def dma(self, q, out, in_, reads=(), writes=()):
        did = len(self.dmas)
        slot = did % N_DSEM
        val = 16 * (did // N_DSEM + 1)
        waits = self._deps(q, reads, writes)
        if val > 16 and self.known_d[q][slot] < val - 16:
            self.known_d[q][slot] = val - 16
            waits.append(("d", slot, val - 16))
        self.dmas.append((slot, val))
        fn = lambda e: e.dma_start(out=out, in_=in_)
        self.ops[q].append([fn, waits, False, did])
        me = ("d", did)
        for t in reads:
            t.rd.append(me)
        for t in writes:
            t.lw = me
            t.rd = []
        return me

    def barrier(self):
        last = {e: len(self.ops[e]) - 1 for e in ENGS}
        ndma = len(self.dmas)
        for e in ENGS:
            waits = []
            for oe in ENGS:
                if oe == e:
                    continue
                idx = last[oe]
                while idx >= 0 and (self.ops[oe][idx][3] is not None or self.ops[oe][idx][0] is None):
                    idx -= 1
                if idx >= 0 and self.known[e][oe] < idx:
                    self.known[e][oe] = idx
                    self.ops[oe][idx][2] = True
                    waits.append(("e", oe, idx))
            for did in range(max(0, ndma - N_DSEM), ndma):
                slot, val = self.dmas[did]
                if self.known_d[e][slot] < val:
                    self.known_d[e][slot] = val
                    waits.append(("d", slot, val))
            self.ops[e].append([None, waits, False, None])

    def emit(self):
        nc = self.nc
        with ExitStack() as es:
            esem = {e: es.enter_context(nc.semaphore("s_" + e)) for e in ENGS}
            dsem = [es.enter_context(nc.semaphore("d%d" % i)) for i in range(N_DSEM)]
            val = {}
            for e in ENGS:
                c = 0
                for i, o in enumerate(self.ops[e]):
                    if o[2]:
                        c += 1
                        val[(e, i)] = c
            block = es.enter_context(nc.Block())

            def run(e, eng):
                for i, (fn, waits, marked, did) in enumerate(self.ops[e]):
                    for w in waits:
                        if w[0] == "e":
                            eng.wait_ge(esem[w[1]], val[(w[1], w[2])])
                        else:
                            eng.wait_ge(dsem[w[1]], w[2])
                    if fn is None:
                        continue
                    ins = fn(eng)
                    if did is not None:
                        ins.then_inc(dsem[self.dmas[did][0]], 16)
                    elif marked:
                        ins.then_inc(esem[e], 1)

            @block.tensor
            def _(eng):
                run("pe", eng)

            @block.scalar
            def _(eng):
                run("act", eng)

            @block.vector
            def _(eng):
                run("dve", eng)

            @block.gpsimd
            def _(eng):
                run("pool", eng)

            @block.sync
            def _(eng):
                run("sp", eng)


class Buf:
    def __init__(self, t, name=""):
        self.t = t
        self.T = T(name)

    def __getitem__(self, k):
        return self.t[k]


class Ring:
    def __init__(self, bufs):
        self.bufs = bufs
        self.i = 0

    def next(self):
        b = self.bufs[self.i % len(self.bufs)]
        self.i += 1
        return b


def fap(ap, dims, off=0):
    a = ap.ap
    return bass.AP(ap.tensor, ap.offset + off, [list(a[0])] + [list(d) for d in dims])


def build(TL=4096, TC=256, DEPTH=4, dbg=False, stop_after=None):
    nc = bass.Bass("TRN2", target_bir_lowering=False)
    ROWS = TL // 64
    TT = TC + TL
    P = Prog(nc)

    def din(name, shape, dt=F32):
        return nc.dram_tensor(name, list(shape), dt, kind="ExternalInput").ap()

    def dscr(name, shape, dt=F32):
        return nc.dram_tensor(name, list(shape), dt, kind="ExternalOutput" if dbg else "Internal").ap()

    xT = din("xT", [D, TL])
    cxT = din("cxT", [D, TC])
    cs_d = din("cs", [128, KD, 2])
    w_ada = din("w_ada", [DEPTH, D, 3 * D])
    badaT = din("badaT", [128, DEPTH, 48])
    normgT = din("normgT", [128, DEPTH, KD])
    w_in = din("w_in", [DEPTH, D, DIN])
    b_in = din("b_in", [DEPTH, DIN])
    bfm_d = din("bfm", [128, DEPTH, NFM])
    wsT_d = din("wsT", [DEPTH, 128, 4, 128])
    bs_d = din("bs", [DEPTH, 512])
    convT_d = din("convT", [128, DEPTH, 16, 3])
    mnT_d = din("mnT", [128, DEPTH, 8])
    hnT_d = din("hnT", [128, DEPTH, 4])
    lbT_d = din("lbT", [128, 4, DEPTH])
    fnT_d = din("fnT", [128, KD])
    w_out = din("w_out", [DEPTH, D, D])
    outT = nc.dram_tensor("outT", [D, TL], F32, kind="ExternalOutput").ap()

    hT = {0: dscr("hlT", [D, TL]), 1: dscr("hcT", [D, TC])}
    uzT = dscr("uzT", [512, TT], BF16)
    qpT = dscr("qpT", [1024, TT], BF16)
    kpT = dscr("kpT", [1024, TT], BF16)
    gBT = dscr("gBT", [1024, TT], BF16)
    vn_d = dscr("vn", [TT, 512], BF16)
    vb_d = dscr("vb", [TT, 1024], BF16)
    gts_d = dscr("gts", [TT, 16], F32)
    cqT = dscr("cqT", [512, TT], BF16)
    ckT = [dscr("ckT%d" % i, [512, TT], BF16) for i in range(2)]
    lfT = [dscr("lfT%d" % i, [512, TT], F32) for i in range(2)]
    gCT = dscr("gCT", [512, TT], BF16)
    ci_d = dscr("ci", [TT, 512], BF16)
    ymT = dscr("ymT", [1536, TT], BF16)
    qcT = dscr("qcT", [1024, TT], BF16)
    kcT = dscr("kcT", [1024, TT], BF16)
    ktm_d = dscr("ktm", [TT, 1024], BF16)
    hf_d = dscr("hf", [TT, 1024], F32)
    hbk_d = dscr("hbk", [TT, 1024], F32)
    ofT = dscr("ofT", [512, TT], F32)

    with ExitStack() as glob:
        uid = [0]

        def sb(es, name, shape, dt=F32):
            uid[0] += 1
            return Buf(es.enter_context(nc.sbuf_tensor("sb%d_%s" % (uid[0], name), list(shape), dt)), name)

        def ps(es, name, shape, dt=F32):
            uid[0] += 1
            return Buf(es.enter_context(nc.psum_tensor("ps%d_%s" % (uid[0], name), list(shape), dt)), name)

        ident = sb(glob, "ident", [128, 128])
        ones_bf = sb(glob, "ones_bf", [128, 128], BF16)
        ones_f = sb(glob, "ones_f", [128, 128])
        tri_f = sb(glob, "tri_f", [128, 128])
        tri_b = sb(glob, "tri_b", [128, 128])
        epsc = sb(glob, "epsc", [128, 1])
        modT = sb(glob, "modT", [128, DEPTH, 2, 48])
        gsT = sb(glob, "gsT", [128, DEPTH, 2, KD])
        normg = sb(glob, "normg", [128, DEPTH, KD])
        bfm = sb(glob, "bfm", [128, DEPTH, NFM])
        convT = sb(glob, "convT", [128, DEPTH, 16, 3])
        mnT = sb(glob, "mnT", [128, DEPTH, 8])
        hnT = sb(glob, "hnT", [128, DEPTH, 4])
        fnT = sb(glob, "fnT", [128, KD])
        lb = sb(glob, "lb", [128, 4, DEPTH])
        oml = sb(glob, "oml", [128, 4, DEPTH])
        noml = sb(glob, "noml", [128, 4, DEPTH])
        psb = [ps(glob, "psb%d" % i, [128, 512]) for i in range(8)]
        css = sb(glob, "css", [128, KD, 2])
        bada = sb(glob, "bada", [128, DEPTH, 48])

        def act(fn, reads, writes):
            return P.op("act", fn, [b.T for b in reads], [b.T for b in writes])

        def dve(fn, reads, writes):
            return P.op("dve", fn, [b.T for b in reads], [b.T for b in writes])

        def pool(fn, reads, writes):
            return P.op("pool", fn, [b.T for b in reads], [b.T for b in writes])

        def pe(fn, reads, writes):
            return P.op("pe", fn, [b.T for b in reads], [b.T for b in writes])

        def dma(q, out, in_, reads=(), writes=()):
            return P.dma(q, out, in_, [b.T for b in reads], [b.T for b in writes])

        def mm(out, lhsT, rhs, start, stop, reads, writes):
            pe(lambda e: e.matmul(out, lhsT=lhsT, rhs=rhs, start=start, stop=stop), reads, writes)

        pool(lambda e: e.memset(ident[:], 0.0), [], [ident])
        pool(lambda e: e.affine_select(out=ident[:], in_=ident[:], pattern=[[-1, 128]], compare_op=ALU.not_equal,
                                       fill=1.0, base=0, channel_multiplier=1), [ident], [ident])
        pool(lambda e: e.memset(ones_bf[:], 1.0), [], [ones_bf])
        pool(lambda e: e.memset(ones_f[:], 1.0), [], [ones_f])
        pool(lambda e: e.memset(tri_f[:], 1.0), [], [tri_f])
        pool(lambda e: e.affine_select(out=tri_f[:], in_=tri_f[:], pattern=[[1, 128]], compare_op=ALU.is_ge,
                                       fill=0.0, base=0, channel_multiplier=-1), [tri_f], [tri_f])
        pool(lambda e: e.memset(tri_b[:], 1.0), [], [tri_b])
        pool(lambda e: e.affine_select(out=tri_b[:], in_=tri_b[:], pattern=[[-1, 128]], compare_op=ALU.is_ge,
                                       fill=0.0, base=0, channel_multiplier=1), [tri_b], [tri_b])
        pool(lambda e: e.memset(epsc[:], EPS), [], [epsc])
        dma("sp", normg[:], normgT, [], [normg])
        dma("sp", bfm[:], bfm_d, [], [bfm])
        dma("sp", convT[:], convT_d, [], [convT])
        dma("sp", mnT[:], mnT_d, [], [mnT])
        dma("sp", hnT[:], hnT_d, [], [hnT])
        dma("sp", fnT[:], fnT_d, [], [fnT])

        with ExitStack() as es:
            lgt = sb(es, "lgt", [128, 4, DEPTH])
            lsum = sb(es, "lsum", [128, 4])
            dma("sp", css[:], cs_d, [], [css])
            dma("sp", bada[:], badaT, [], [bada])
            dma("sp", lgt[:], lbT_d, [], [lgt])
            act(lambda e: e.activation(out=css[:], in_=css[:], func=AF.Silu), [css], [css])
            act(lambda e: e.activation(out=lgt[:], in_=lgt[:], func=AF.Exp), [lgt], [lgt])
            dve(lambda e: e.tensor_reduce(out=lsum[:], in_=lgt[:], axis=AX.X, op=ALU.add), [lgt], [lsum])
            dve(lambda e: e.reciprocal(out=lsum[:], in_=lsum[:]), [lsum], [lsum])
            dve(lambda e: e.tensor_tensor(out=lgt[:], in0=lgt[:], in1=fap(lsum[:], [[1, 4], [0, DEPTH]]),
                                          op=ALU.mult), [lgt, lsum], [lgt])
            dve(lambda e: e.memset(lb[:, :, 0:1], 0.0), [], [lb])
            for l in range(1, DEPTH):
                dve(lambda e, l=l: e.tensor_tensor(out=lb[:, :, l:l + 1], in0=lb[:, :, l - 1:l], in1=lgt[:, :, l:l + 1],
                                                   op=ALU.add), [lb, lgt], [lb])
            dve(lambda e: e.tensor_scalar(out=oml[:], in0=lb[:], scalar1=-1.0, scalar2=1.0, op0=ALU.mult, op1=ALU.add),
                [lb], [oml])
            dve(lambda e: e.tensor_scalar(out=noml[:], in0=lb[:], scalar1=1.0, scalar2=-1.0, op0=ALU.mult, op1=ALU.add),
                [lb], [noml])
            pm = psb[0]
            P.barrier()

        def compute_mod(l, es):
            wa = [sb(es, "wa%d_%d" % (l, i), [128, KD, 512]) for i in range(2)]
            for jg in range(12):
                w = wa[jg % 2]
                src = w_ada[l, :, jg * 512:(jg + 1) * 512].rearrange("(k p) f -> p k f", p=128)
                dma("sp" if jg % 2 == 0 else "act", w[:], src, [], [w])
                for jj in range(4):
                    j = jg * 4 + jj
                    for k in range(KD):
                        mm(pm[:, j * 2:(j + 1) * 2], w[:, k, jj * 128:(jj + 1) * 128], css[:, k, :],
                           k == 0, k == KD - 1, [w, css], [pm])
                yield
            dve(lambda e: e.tensor_tensor(
                out=fap(modT[:], [[1, 48], [48, 2]], off=l * 96),
                in0=fap(pm[:], [[2, 48], [1, 2]]),
                in1=fap(bada[:], [[1, 48], [0, 2]], off=l * 48), op=ALU.add), [pm, bada], [modT])
            for s_ in range(2):
                dve(lambda e, s_=s_: e.scalar_tensor_tensor(
                    out=gsT[:, l, s_, :], in0=modT[:, l, s_, 16:32], scalar=1.0, in1=normg[:, l, :],
                    op0=ALU.add, op1=ALU.mult), [modT, normg], [gsT])

        with ExitStack() as es0:
            for _ in compute_mod(0, es0):
                pass
            P.barrier()

        if dbg:
            dbg_mod = nc.dram_tensor("dbg_mod", [128, DEPTH * 96], F32, kind="ExternalOutput").ap()
            dbg_lb = nc.dram_tensor("dbg_lb", [128, 4 * DEPTH], F32, kind="ExternalOutput").ap()
            dma("sp", dbg_mod, modT[:].rearrange("p l s j -> p (l s j)"), [modT], [])
            dma("sp", dbg_lb, lb[:].rearrange("p c l -> p (c l)"), [lb], [])

        def blocks(s):
            if s == 1:
                return [(t0, min(512, TC - t0)) for t0 in range(0, TC, 512)]
            return [(t0, 512) for t0 in range(0, TL, 512)]
        GOFF = {1: 0, 0: TC}
        all_blocks = [(1, t0, n) for (t0, n) in blocks(1)] + [(0, t0, n) for (t0, n) in blocks(0)]

        def layer(l):
            last = l == DEPTH - 1
            src_h = {0: xT, 1: cxT} if l == 0 else hT
            with ExitStack() as lay:
                nT = sb(lay, "nT", [128, KD + 1, TT], BF16)
                slotT = [Buf(None, "slot%d" % i) for i in range(KD + 1)]
                phys = list(range(KD))
                spare = [KD]
                with ExitStack() as es:
                    hin = [sb(es, "hin%d" % i, [128, KD, 256]) for i in range(2)]
                    sq = sb(es, "sq", [128, KD, 256], BF16)
                    rstd = sb(es, "rstd", [128, 256])
                    ut = [sb(es, "ut%d" % i, [128, 256]) for i in range(4)]
                    nblocks = [(s_, t0_ + o_, min(256, n_ - o_)) for (s_, t0_, n_) in all_blocks for o_ in range(0, n_, 256)]
                    for bi, (s, t0, n) in enumerate(nblocks):
                        h = hin[bi % 2]
                        dma("sp", h[:, :, 0:n], src_h[s][:, t0:t0 + n].rearrange("(k p) t -> p k t", p=128), [], [h])
                        act(lambda e, h=h, n=n: e.activation(out=sq[:, :, 0:n], in_=h[:, :, 0:n], func=AF.Square), [h], [sq])
                        pss = psb[bi % 2]
                        for k in range(KD):
                            mm(pss[:, 0:n], ones_bf[:], sq[:, k, 0:n], k == 0, k == KD - 1, [ones_bf, sq], [pss])
                        act(lambda e, pss=pss, n=n: e.activation(out=rstd[:, 0:n], in_=pss[:, 0:n], func=AF.Sqrt,
                                                                 bias=epsc[:], scale=1.0 / D), [pss, epsc], [rstd])
                        dve(lambda e, n=n: e.reciprocal(out=rstd[:, 0:n], in_=rstd[:, 0:n]), [rstd], [rstd])
                        g0 = GOFF[s] + t0
                        for k in range(KD):
                            u = ut[k % 4]
                            dve(lambda e, u=u, h=h, k=k, n=n: e.tensor_tensor(out=u[:, 0:n], in0=h[:, k, 0:n], in1=rstd[:, 0:n],
                                                                              op=ALU.mult), [h, rstd], [u])
                            if k % 2 == 0:
                                act(lambda e, u=u, k=k, n=n, g0=g0, s=s: e.activation(
                                    out=nT[:, k, g0:g0 + n], in_=u[:, 0:n], func=AF.Identity,
                                    bias=modT[:, l, s, k:k + 1], scale=gsT[:, l, s, k:k + 1]), [u, modT, gsT], [slotT[k]])
                            else:
                                pool(lambda e, u=u, k=k, n=n, g0=g0, s=s: e.tensor_scalar(
                                    out=nT[:, k, g0:g0 + n], in0=u[:, 0:n], scalar1=gsT[:, l, s, k:k + 1],
                                    scalar2=modT[:, l, s, k:k + 1], op0=ALU.mult, op1=ALU.add), [u, modT, gsT], [slotT[k]])
                    P.barrier()
                if dbg and l == 0:
                    dbg_nT = nc.dram_tensor("dbg_nT", [128, KD, TT], BF16, kind="ExternalOutput").ap()
                    dma("sp", dbg_nT, nT[:, 0:KD, :], slotT, [])
                if stop_after == "N":
                    return True

                def n_nat(k, s, t0, n):
                    g0 = GOFF[s] + t0
                    return nT[:, phys[k], g0:g0 + n]

                n_scan = n_nat

                with ExitStack() as es:
                    wt = Ring([sb(es, "wt%d" % i, [128, KD, 256], BF16) for i in range(3)])
                    st32 = Ring([sb(es, "st32_%d" % i, [128, 512]) for i in range(2)])
                    st16 = Ring([sb(es, "st16_%d" % i, [128, 512], BF16) for i in range(4)])
                    tmp32 = Ring([sb(es, "tmp32_%d" % i, [128, 512]) for i in range(3)])
                    bias_tm = sb(es, "bias_tm", [128, 2064])
                    stat = Ring([sb(es, "stat%d" % i, [128, 8]) for i in range(4)])
                    pring = Ring(psb)
                    tm_off = {C_AV: 0, C_BV: 512, C_GT: 1536, C_CI: 1552}
                    for c0, n in ((C_AV, 512), (C_BV, 1024), (C_GT, 16), (C_CI, 512)):
                        dma("sp", bias_tm[:, tm_off[c0]:tm_off[c0] + n],
                            b_in[l:l + 1, c0:c0 + n].partition_broadcast(128), [], [bias_tm])

                    def load_w(c0, n):
                        w = wt.next()
                        dma("pool", w[:, :, 0:n], w_in[l, :, c0:c0 + n].rearrange("(k p) f -> p k f", p=128), [], [w])
                        return w

                    def fm_mm(w, sub, nfun, s, t0, n):
                        pb = pring.next()
                        for k in range(KD):
                            mm(pb[:, 0:n], w[:, k, sub * 128:(sub + 1) * 128], nfun(k, s, t0, n), k == 0, k == KD - 1,
                               [w, slotT[phys[k]]], [pb])
                        return pb

                    def bias_ap(c0):
                        return bfm[:, l, FM_IDX[c0]:FM_IDX[c0] + 1]

                    def store(q, dst, src_ap, buf):
                        dma(q, dst, src_ap, [buf], [])

                    for j in range(2):
                        wu = load_w(C_AU + j * 256, 256)
                        wz = load_w(C_AZ + j * 256, 256)
                        for sub in range(2):
                            f0 = j * 256 + sub * 128
                            for (s, t0, n) in all_blocks:
                                if last and s == 1:
                                    continue
                                g0 = GOFF[s] + t0
                                pz = fm_mm(wz, sub, n_nat, s, t0, n)
                                pu = fm_mm(wu, sub, n_nat, s, t0, n)
                                zs = tmp32.next()
                                act(lambda e, zs=zs, pz=pz, n=n, f0=f0: e.activation(
                                    out=zs[:, 0:n], in_=pz[:, 0:n], func=AF.Silu, bias=bias_ap(C_AZ + f0)), [pz, bfm], [zs])
                                o = st16.next()
                                dve(lambda e, o=o, pu=pu, zs=zs, n=n, f0=f0: e.scalar_tensor_tensor(
                                    out=o[:, 0:n], in0=pu[:, 0:n], scalar=bias_ap(C_AU + f0), in1=zs[:, 0:n],
                                    op0=ALU.add, op1=ALU.mult), [pu, zs, bfm], [o])
                                store("sp", uzT[f0:f0 + 128, g0:g0 + n], o[:, 0:n], o)
                    def fm_simple(c_base, ncols, dst, func, nfun, skip_ctx_last=False):
                        for j in range(ncols // 256):
                            w = load_w(c_base + j * 256, 256)
                            for sub in range(2):
                                f0 = j * 256 + sub * 128
                                for (s, t0, n) in all_blocks:
                                    g0 = GOFF[s] + t0
                                    pb = fm_mm(w, sub, nfun, s, t0, n)
                                    o = st16.next()
                                    act(lambda e, o=o, pb=pb, n=n, f0=f0: e.activation(
                                        out=o[:, 0:n], in_=pb[:, 0:n], func=func, bias=bias_ap(c_base + f0)), [pb, bfm], [o])
                                    store("sp", dst[f0:f0 + 128, g0:g0 + n], o[:, 0:n], o)
                    fm_simple(C_BQ, 1024, qpT, AF.Identity, n_nat)
                    fm_simple(C_BK, 1024, kpT, AF.Identity, n_nat)
                    for j in range(4):
                        wo = load_w(C_BO + j * 256, 256)
                        wz = load_w(C_BZ + j * 256, 256)
                        for sub in range(2):
                            f0 = j * 256 + sub * 128
                            for (s, t0, n) in all_blocks:
                                if last and s == 1:
                                    continue
                                g0 = GOFF[s] + t0
                                po = fm_mm(wo, sub, n_nat, s, t0, n)
                                pz = fm_mm(wz, sub, n_nat, s, t0, n)
                                so = tmp32.next()
                                sz = tmp32.next()
                                act(lambda e, so=so, po=po, n=n, f0=f0: e.activation(
                                    out=so[:, 0:n], in_=po[:, 0:n], func=AF.Sigmoid, bias=bias_ap(C_BO + f0)), [po, bfm], [so])
                                act(lambda e, sz=sz, pz=pz, n=n, f0=f0: e.activation(
                                    out=sz[:, 0:n], in_=pz[:, 0:n], func=AF.Silu, bias=bias_ap(C_BZ + f0)), [pz, bfm], [sz])
                                o = st16.next()
                                pool(lambda e, o=o, so=so, sz=sz, n=n: e.tensor_tensor(
                                    out=o[:, 0:n], in0=so[:, 0:n], in1=sz[:, 0:n], op=ALU.mult), [so, sz], [o])
                                store("sp", gBT[f0:f0 + 128, g0:g0 + n], o[:, 0:n], o)
                    def tm_tiles(scan):
                        res = []
                        for (s, t0, n) in all_blocks:
                            for tt in range(0, n, 128):
                                res.append((s, t0 + tt))
                        return res

                    def tm_mm(w, ncols, nfun, s, t0):
                        pb = pring.next()
                        for k in range(KD):
                            lhs = nfun(k, s, t0, 128)
                            mm(pb[:, 0:ncols], lhs, w[:, k, 0:ncols], k == 0, k == KD - 1, [w, slotT[phys[k]]], [pb])
                        return pb

                    for j in range(2):
                        w = load_w(C_AV + j * 256, 256)
                        for (s, t0) in tm_tiles(False):
                            if last and s == 1:
                                continue
                            g0 = GOFF[s] + t0
                            pb = tm_mm(w, 256, n_nat, s, t0)
                            v = tmp32.next()
                            dve(lambda e, v=v, pb=pb, j=j: e.tensor_tensor(
                                out=v[:, 0:256], in0=pb[:, 0:256], in1=bias_tm[:, j * 256:(j + 1) * 256], op=ALU.add),
                                [pb, bias_tm], [v])
                            o = st16.next()
                            for g in range(2):
                                stt = stat.next()
                                dve(lambda e, stt=stt, v=v, g=g: e.bn_stats(out=stt[:, 0:6], in_=v[:, g * 128:(g + 1) * 128]), [v], [stt])
                                dve(lambda e, stt=stt: e.bn_aggr(out=stt[:, 6:8], in_=stt[:, 0:6]), [stt], [stt])
                                act(lambda e, stt=stt: e.activation(out=stt[:, 7:8], in_=stt[:, 7:8], func=AF.Sqrt, bias=epsc[:], scale=1.0),
                                    [stt, epsc], [stt])
                                dve(lambda e, stt=stt: e.reciprocal(out=stt[:, 7:8], in_=stt[:, 7:8]), [stt], [stt])
                                dve(lambda e, stt=stt, v=v, g=g, o=o: e.tensor_scalar(
                                    out=o[:, g * 128:(g + 1) * 128], in0=v[:, g * 128:(g + 1) * 128], scalar1=stt[:, 6:7],
                                    scalar2=stt[:, 7:8], op0=ALU.subtract, op1=ALU.mult), [stt, v], [o])
                            store("sp", vn_d[g0:g0 + 128, j * 256:(j + 1) * 256], o[:, 0:256], o)

                    def tm_simple(c_base, ncols_total, dst, dt, nfun, boff):
                        step = min(256, ncols_total)
                        for j in range(ncols_total // step):
                            w = load_w(c_base + j * step, step)
                            for (s, t0) in tm_tiles(False):
                                g0 = GOFF[s] + t0
                                pb = tm_mm(w, step, nfun, s, t0)
                                o = st16.next() if dt == BF16 else st32.next()
                                dve(lambda e, o=o, pb=pb, j=j: e.tensor_tensor(
                                    out=o[:, 0:step], in0=pb[:, 0:step], in1=bias_tm[:, boff + j * step:boff + (j + 1) * step],
                                    op=ALU.add), [pb, bias_tm], [o])
                                store("sp", dst[g0:g0 + 128, j * step:(j + 1) * step], o[:, 0:step], o)
                    tm_simple(C_BV, 1024, vb_d, BF16, n_nat, 512)
                    tm_simple(C_GT, 16, gts_d, F32, n_nat, 1536)
                    for k in range(KD):
                        src_slot, dst_slot = phys[k], spare[0]
                        engs = ("dve", "pool", "act")
                        eng = engs[k % 3]
                        src_ap = nT[:, src_slot, TC:TT].rearrange("p (r c) -> p c r", c=64)
                        dst_ap = nT[:, dst_slot, TC:TT].rearrange("p (c r) -> p c r", r=ROWS)
                        if eng == "act":
                            fn = lambda e, o=dst_ap, i=src_ap: e.copy(out=o, in_=i)
                            fn2 = lambda e, o=nT[:, dst_slot, 0:TC], i=nT[:, src_slot, 0:TC]: e.copy(out=o, in_=i)
                        else:
                            fn = lambda e, o=dst_ap, i=src_ap: e.tensor_copy(out=o, in_=i)
                            fn2 = lambda e, o=nT[:, dst_slot, 0:TC], i=nT[:, src_slot, 0:TC]: e.tensor_copy(out=o, in_=i)
                        P.op(eng, fn, [slotT[src_slot].T], [slotT[dst_slot].T])
                        P.op(eng, fn2, [slotT[src_slot].T], [slotT[dst_slot].T])
                        phys[k] = dst_slot
                        spare[0] = src_slot
                    fm_simple(C_CQ, 512, cqT, AF.Silu, n_scan)
                    if True:
                        fm_simple(C_CG, 512, gCT, AF.Silu, n_scan)
                    for d, cb in enumerate((C_CFF, C_CFB)):
                        for j in range(2):
                            w = load_w(cb + j * 256, 256)
                            for sub in range(2):
                                f0 = j * 256 + sub * 128
                                ch = f0 // 128
                                for (s, t0, n) in all_blocks:
                                    g0 = GOFF[s] + t0
                                    pb = fm_mm(w, sub, n_scan, s, t0, n)
                                    sg = tmp32.next()
                                    act(lambda e, sg=sg, pb=pb, n=n, f0=f0, cb=cb: e.activation(
                                        out=sg[:, 0:n], in_=pb[:, 0:n], func=AF.Sigmoid, bias=bias_ap(cb + f0)), [pb, bfm], [sg])
                                    ko = st16.next()
                                    pool(lambda e, ko=ko, sg=sg, n=n, ch=ch: e.tensor_scalar(
                                        out=ko[:, 0:n], in0=sg[:, 0:n], scalar1=noml[:, ch, l:l + 1], scalar2=oml[:, ch, l:l + 1],
                                        op0=ALU.mult, op1=ALU.add), [sg, noml, oml], [ko])
                                    store("sp", ckT[d][f0:f0 + 128, g0:g0 + n], ko[:, 0:n], ko)
                                    lo = st32.next()
                                    act(lambda e, lo=lo, sg=sg, n=n, ch=ch: e.activation(
                                        out=lo[:, 0:n], in_=sg[:, 0:n], func=AF.Ln, bias=lb[:, ch, l:l + 1],
                                        scale=oml[:, ch, l:l + 1]), [sg, lb, oml], [lo])
                                    store("sp", lfT[d][f0:f0 + 128, g0:g0 + n], lo[:, 0:n], lo)
                    tm_simple(C_CI, 512, ci_d, BF16, n_scan, 1552)
                    P.barrier()
            if stop_after == "P":
                return True
            GT = lambda s: GOFF[s]
            mask_f, mask_b = tri_f, tri_b

            with ExitStack() as es:
                wsb = sb(es, "wsb", [128, 4, 128], BF16)
                bsr = sb(es, "bsr", [1, 512])
                vt = Ring([sb(es, "vt%d" % i, [128, 512], BF16) for i in range(3)])
                uzr = Ring([sb(es, "uzr%d" % i, [128, 4, 512], BF16) for i in range(2)])
                yar = Ring([sb(es, "yar%d" % i, [128, 4, 512], BF16) for i in range(2)])
                pring = Ring(psb)
                dma("pool", wsb[:], wsT_d[l], [], [wsb])
                dma("sp", bsr[:], bs_d[l:l + 1, :], [], [bsr])
                for (s, t0, n) in all_blocks:
                    if last and s == 1:
                        continue
                    g0 = GOFF[s] + t0
                    uz = uzr.next()
                    ya = yar.next()
                    dma("act", uz[:, :, 0:n], uzT[:, g0:g0 + n].rearrange("(g p) t -> p g t", p=128), [], [uz])
                    for c in range(n // 128):
                        v = vt.next()
                        dma("sp", v[:], vn_d[g0 + c * 128:g0 + (c + 1) * 128, :], [], [v])
                        pb = pring.next()
                        for g in range(4):
                            mm(pb[:, g * 128:(g + 1) * 128], v[:, g * 128:(g + 1) * 128], wsb[:, g, :], True, False, [v, wsb], [pb])
                            mm(pb[:, g * 128:(g + 1) * 128], ones_f[0:1, :], bsr[0:1, g * 128:(g + 1) * 128], False, True,
                               [ones_f, bsr], [pb])
                        dve(lambda e, ya=ya, pb=pb, uz=uz, c=c: e.tensor_tensor(
                            out=ya[:, :, c * 128:(c + 1) * 128], in0=pb[:, :].rearrange("p (g t) -> p g t", g=4),
                            in1=uz[:, :, c * 128:(c + 1) * 128], op=ALU.mult), [pb, uz], [ya])
                    dma("sp", ymT[0:512, g0:g0 + n].rearrange("(g p) t -> p g t", p=128), ya[:, :, 0:n], [ya], [])
                P.barrier()
            if stop_after == "A":
                return True

            with ExitStack() as es:
                xin = Ring([sb(es, "xin%d" % i, [128, 8, 514], BF16) for i in range(2)])
                acc = Ring([sb(es, "acc%d" % i, [128, 512]) for i in range(3)])
                xo = Ring([sb(es, "xo%d" % i, [128, 8, 512], BF16) for i in range(2)])
                kt = Ring([sb(es, "kt%d" % i, [128, 4, 1024], BF16) for i in range(2)])
                ident_bf = sb(es, "ident_bf", [128, 128], BF16)
                pool(lambda e: e.tensor_copy(out=ident_bf[:], in_=ident[:]), [ident], [ident_bf])
                pring = Ring(psb[1:8])
                modgen = compute_mod(l + 1, es) if not last else iter(())
                for (s, t0, n) in all_blocks:
                    g0 = GOFF[s] + t0
                    ns = TC if s == 1 else TL
                    for qk, (src, dst) in enumerate(((qpT, qcT), (kpT, kcT))):
                        x = xin.next()
                        lo = 1 if t0 > 0 else 0
                        hi = 1 if t0 + n < ns else 0
                        if not lo:
                            pool(lambda e, x=x: e.memset(x[:, :, 0:1], 0.0), [], [x])
                        if not hi:
                            pool(lambda e, x=x, n=n: e.memset(x[:, :, n + 1:n + 2], 0.0), [], [x])
                        dma("sp" if qk == 0 else "act", x[:, :, 1 - lo:n + 1 + hi],
                            src[:, g0 - lo:g0 + n + hi].rearrange("(c p) t -> p c t", p=128), [], [x])
                        o = xo.next()
                        for c in range(8):
                            a = acc.next()
                            cc = qk * 8 + c
                            act(lambda e, a=a, x=x, c=c, cc=cc, n=n: e.activation(
                                out=a[:, 0:n], in_=x[:, c, 0:n], func=AF.Identity, scale=convT[:, l, cc, 0:1]),
                                [x, convT], [a])
                            dve(lambda e, a=a, x=x, c=c, cc=cc, n=n: e.scalar_tensor_tensor(
                                out=a[:, 0:n], in0=x[:, c, 1:n + 1], scalar=convT[:, l, cc, 1:2], in1=a[:, 0:n],
                                op0=ALU.mult, op1=ALU.add), [x, convT, a], [a])
                            dve(lambda e, a=a, x=x, c=c, cc=cc, n=n: e.scalar_tensor_tensor(
                                out=a[:, 0:n], in0=x[:, c, 2:n + 2], scalar=convT[:, l, cc, 2:3], in1=a[:, 0:n],
                                op0=ALU.mult, op1=ALU.add), [x, convT, a], [a])
                            act(lambda e, a=a, o=o, c=c, n=n: e.activation(out=o[:, c, 0:n], in_=a[:, 0:n], func=AF.Silu), [a], [o])
                        dma("sp", dst[:, g0:g0 + n].rearrange("(c p) t -> p c t", p=128), o[:, :, 0:n], [o], [])
                        if qk == 1:
                            ktb = kt.next()
                            for cch in range(n // 128):
                                for half in range(2):
                                    pb = pring.next()
                                    pbv = pb[:, :].bitcast(BF16)
                                    for c4 in range(4):
                                        c = half * 4 + c4
                                        pe(lambda e, pbv=pbv, o=o, c=c, c4=c4, cch=cch: e.transpose(
                                            out=pbv[:, c4 * 128:(c4 + 1) * 128], in_=o[:, c, cch * 128:(cch + 1) * 128],
                                            identity=ident_bf[:]), [o, ident_bf], [pb])
                                    act(lambda e, ktb=ktb, pbv=pbv, cch=cch, half=half: e.copy(
                                        out=ktb[:, cch, half * 512:(half + 1) * 512], in_=pbv[:, 0:512]), [pb], [ktb])
                            dma("act", ktm_d[g0:g0 + n, :].rearrange("(c p) f -> p c f", p=128), ktb[:, 0:n // 128, :], [ktb], [])
                        next(modgen, None)
                for _ in modgen:
                    pass
                P.barrier()
            if stop_after == "B1":
                return True

            LN16 = float(np.log(16.0))
            with ExitStack() as es:
                nchs = {1: TC // 128, 0: TL // 128}
                G = {s: sb(es, "G%d" % s, [128, nchs[s], 16]) for s in (0, 1)}
                NLF = {s: sb(es, "NLF%d" % s, [128, nchs[s], 2, 4]) for s in (0, 1)}
                EA = {s: sb(es, "EA%d" % s, [128, nchs[s], 2, 4]) for s in (0, 1)}
                EK = {s: sb(es, "EK%d" % s, [128, nchs[s], 2, 4]) for s in (0, 1)}
                EFL = {s: sb(es, "EFL%d" % s, [128, nchs[s], 2, 4]) for s in (0, 1)}
                ECL = {s: sb(es, "ECL%d" % s, [128, nchs[s], 2, 4]) for s in (0, 1)}
                negln16 = sb(es, "negln16", [128, 1])
                onec = sb(es, "onec", [128, 1])
                pool(lambda e: e.memset(negln16[:], -LN16), [], [negln16])
                pool(lambda e: e.memset(onec[:], 1.0), [], [onec])
                for s in (1, 0):
                    nch = nchs[s]
                    dma("sp", G[s][:], gts_d[GOFF[s]:GOFF[s] + nch * 128, :].rearrange("(c p) f -> p c f", p=128), [], [G[s]])
                    fgv = fap(G[s][:], [[16, nch], [8, 2], [1, 4]], off=4)
                    igv = fap(G[s][:], [[16, nch], [8, 2], [1, 4]], off=0)
                    act(lambda e, s=s, fgv=fgv: e.activation(out=NLF[s][:], in_=fgv, func=AF.Exp, scale=-1.0), [G[s]], [NLF[s]])
                    act(lambda e, s=s: e.activation(out=NLF[s][:], in_=NLF[s][:], func=AF.Ln, bias=onec[:], scale=1.0),
                        [NLF[s], onec], [NLF[s]])
                    pg = psb[0] if s == 0 else psb[1]
                    for c in range(nch):
                        mm(pg[:, c * 16:c * 16 + 4], tri_f[:], NLF[s][:, c, 0, :], True, True, [tri_f, NLF[s]], [pg])
                        mm(pg[:, c * 16 + 4:c * 16 + 8], tri_b[:], NLF[s][:, c, 1, :], True, True, [tri_b, NLF[s]], [pg])
                        mm(pg[:, c * 16 + 8:c * 16 + 16], ones_f[:], fap(NLF[s][:], [[1, 8]], off=c * 8), True, True, [ones_f, NLF[s]], [pg])
                    ncb = fap(pg[:], [[16, nch], [4, 2], [1, 4]], off=0)
                    ncl = fap(pg[:], [[16, nch], [4, 2], [1, 4]], off=8)
                    dve(lambda e, s=s, igv=igv, ncb=ncb: e.tensor_tensor(out=EA[s][:], in0=ncb, in1=igv, op=ALU.add), [pg, G[s]], [EA[s]])
                    dve(lambda e, s=s, ncl=ncl: e.tensor_tensor(out=EK[s][:], in0=EA[s][:], in1=ncl, op=ALU.subtract), [pg, EA[s]], [EK[s]])
                    act(lambda e, s=s: e.activation(out=EA[s][:], in_=EA[s][:], func=AF.Exp, bias=negln16[:], scale=1.0), [EA[s], negln16], [EA[s]])
                    act(lambda e, s=s: e.activation(out=EK[s][:], in_=EK[s][:], func=AF.Exp, bias=negln16[:], scale=1.0), [EK[s], negln16], [EK[s]])
                    act(lambda e, s=s, ncb=ncb: e.activation(out=EFL[s][:], in_=ncb, func=AF.Exp), [pg], [EFL[s]])
                    act(lambda e, s=s, ncl=ncl: e.activation(out=ECL[s][:], in_=ncl, func=AF.Exp, scale=-1.0), [pg], [ECL[s]])
                P.barrier()
                if dbg and l == 0:
                    for nm, tt in (("EA", EA), ("EK", EK), ("EFL", EFL), ("ECL", ECL), ("NLF", NLF)):
                        for s_ in (0, 1):
                            dd = nc.dram_tensor("dbg_%s%d" % (nm, s_), [128, nchs[s_], 2, 4], F32, kind="ExternalOutput").ap()
                            dma("sp", dd, tt[s_][:], [tt[s_]], [])

                qb = Ring([sb(es, "qb%d" % i, [128, 8, 512], BF16) for i in range(3)])
                kb = Ring([sb(es, "kb%d" % i, [128, 8, 512], BF16) for i in range(3)])
                ktmb = Ring([sb(es, "ktmb%d" % i, [128, 4, 1024], BF16) for i in range(3)])
                vbb = Ring([sb(es, "vbb%d" % i, [128, 4, 1024], BF16) for i in range(3)])
                Wt = Ring([sb(es, "Wt%d" % i, [128, 4, 128]) for i in range(3)])
                Sd = Ring([sb(es, "Sd%d" % i, [128, 512], BF16) for i in range(3)])
                k2 = Ring([sb(es, "k2_%d" % i, [128, 1024], BF16) for i in range(3)])
                hbuf = Ring([sb(es, "hbuf%d" % i, [128, 1024]) for i in range(3)])
                sm = Ring([sb(es, "sm%d" % i, [128, 16]) for i in range(4)])
                C32r = Ring([sb(es, "C32_%d" % i, [128, 4, 2, 256]) for i in range(2)])
                Cbr = Ring([sb(es, "Cb_%d" % i, [128, 4, 2, 256], BF16) for i in range(2)])
                n32r = Ring([sb(es, "n32_%d" % i, [128, 4, 2]) for i in range(2)])
                nbr = Ring([sb(es, "nb_%d" % i, [128, 4, 2], BF16) for i in range(2)])
                pS, pn0, pn1, pD = psb[0], psb[1], psb[2], psb[3]
                pC = psb[4:8]

                def bdir(d):
                    st = {"C32": C32r.next(), "n32": n32r.next(), "Cb": Cbr.next(), "nb": nbr.next()}
                    dve(lambda e, t_=st["C32"]: e.memset(t_[:], 0.0), [], [st["C32"]])
                    dve(lambda e, t_=st["n32"]: e.memset(t_[:], 0.0), [], [st["n32"]])
                    pool(lambda e, t_=st["Cb"]: e.memset(t_[:], 0.0), [], [st["Cb"]])
                    pool(lambda e, t_=st["nb"]: e.memset(t_[:], 0.0), [], [st["nb"]])
                    maskd = tri_f if d == 0 else tri_b
                    hdst = hf_d if d == 0 else hbk_d
                    blks = list(all_blocks)
                    if d == 1:
                        blks = [b_ for b_ in blks if b_[0] == 1][::-1] + [b_ for b_ in blks if b_[0] == 0][::-1]

                    def load_block(bk):
                        s, t0, n = bk
                        g0 = GOFF[s] + t0
                        ncb_ = n // 128
                        q_ = qb.next(); k_ = kb.next(); ktm_ = ktmb.next(); v_ = vbb.next()
                        dma("sp", q_[:, :, 0:n], qcT[:, g0:g0 + n].rearrange("(c p) t -> p c t", p=128), [], [q_])
                        dma("act", k_[:, :, 0:n], kcT[:, g0:g0 + n].rearrange("(c p) t -> p c t", p=128), [], [k_])
                        dma("sp", ktm_[:, 0:ncb_, :], ktm_d[g0:g0 + n, :].rearrange("(c p) f -> p c f", p=128), [], [ktm_])
                        dma("act", v_[:, 0:ncb_, :], vb_d[g0:g0 + n, :].rearrange("(c p) f -> p c f", p=128), [], [v_])
                        return (q_, k_, ktm_, v_)

                    def chunk_gen():
                        bufs = load_block(blks[0])
                        for bi, (s, t0, n) in enumerate(blks):
                            nxt_bufs = None
                            ncb_ = n // 128
                            corder = list(range(ncb_) if d == 0 else range(ncb_ - 1, -1, -1))
                            for ci, c in enumerate(corder):
                                if ci == 0 and bi + 1 < len(blks):
                                    nxt_bufs = load_block(blks[bi + 1])
                                yield dict(s=s, c=c, cg=t0 // 128 + c, tg=GOFF[s] + t0 + c * 128, bufs=bufs)
                            bufs = nxt_bufs

                    def prep(ch):
                        s, c, cg = ch["s"], ch["c"], ch["cg"]
                        q_, k_, ktm_, v_ = ch["bufs"]
                        cs_ = slice(c * 128, (c + 1) * 128)
                        W = Wt.next()
                        pool(lambda e, W=W, s=s, cg=cg: e.tensor_tensor(
                            out=W[:], in0=fap(maskd[:], [[0, 4], [1, 128]]),
                            in1=fap(EA[s][:], [[1, 4], [0, 128]], off=cg * 8 + d * 4), op=ALU.mult), [maskd, EA[s]], [W])
                        kk = k2.next()
                        pool(lambda e, kk=kk, ktm_=ktm_, c=c, s=s, cg=cg: e.tensor_tensor(
                            out=kk[:].rearrange("p (h e) -> p h e", h=4), in0=ktm_[:, c, :].rearrange("p (h e) -> p h e", h=4),
                            in1=fap(EK[s][:], [[1, 4], [0, 256]], off=cg * 8 + d * 4), op=ALU.mult), [ktm_, EK[s]], [kk])
                        for h in range(4):
                            for dh in range(2):
                                mm(pS[:, h * 128:(h + 1) * 128], k_[:, h * 2 + dh, cs_], q_[:, h * 2 + dh, cs_], dh == 0, dh == 1,
                                   [k_, q_], [pS])
                        sd = Sd.next()
                        dve(lambda e, sd=sd, W=W: e.tensor_tensor(out=sd[:], in0=pS[:], in1=W[:].rearrange("p h t -> p (h t)"),
                                                                  op=ALU.mult), [pS, W], [sd])
                        return (kk, sd)

                    def main(ch, pr):
                        s, c, cg, tg = ch["s"], ch["c"], ch["cg"], ch["tg"]
                        q_, k_, ktm_, v_ = ch["bufs"]
                        kk, sd = pr
                        cs_ = slice(c * 128, (c + 1) * 128)
                        for h in range(4):
                            for dh in range(2):
                                mm(pC[h][:, dh * 256:(dh + 1) * 256], kk[:, h * 256 + dh * 128:h * 256 + (dh + 1) * 128],
                                   v_[:, c, h * 256:(h + 1) * 256], True, True, [kk, v_], [pC[h]])
                        for h in range(4):
                            for dh in range(2):
                                mm(pD[:, 8 + h * 2 + dh:9 + h * 2 + dh], kk[:, h * 256 + dh * 128:h * 256 + (dh + 1) * 128],
                                   ones_bf[:, 0:1], True, True, [kk, ones_bf], [pD])
                        Cold, nold, Cbold, nbold = st["C32"], st["n32"], st["Cb"], st["nb"]
                        Cnew, nnew, Cbnew, nbnew = C32r.next(), n32r.next(), Cbr.next(), nbr.next()
                        for h in range(4):
                            dve(lambda e, h=h, Cold=Cold, Cnew=Cnew: e.scalar_tensor_tensor(
                                out=Cnew[:, h, :, :].rearrange("p a b -> p (a b)"), in0=Cold[:, h, :, :].rearrange("p a b -> p (a b)"),
                                scalar=ECL[s][:, cg, d, h:h + 1], in1=pC[h][:, :], op0=ALU.mult, op1=ALU.add),
                                [Cold, ECL[s], pC[h]], [Cnew])
                        pool(lambda e, nold=nold, nnew=nnew: e.tensor_tensor(
                            out=nnew[:], in0=nold[:], in1=fap(ECL[s][:], [[1, 4], [0, 2]], off=cg * 8 + d * 4), op=ALU.mult),
                            [nold, ECL[s]], [nnew])
                        dve(lambda e, nnew=nnew: e.tensor_tensor(out=nnew[:], in0=nnew[:], in1=fap(pD[:], [[2, 4], [1, 2]], off=8), op=ALU.add),
                            [nnew, pD], [nnew])
                        act(lambda e, Cbnew=Cbnew, Cnew=Cnew: e.copy(out=Cbnew[:], in_=Cnew[:]), [Cnew], [Cbnew])
                        pool(lambda e, nbnew=nbnew, nnew=nnew: e.tensor_copy(out=nbnew[:], in_=nnew[:]), [nnew], [nbnew])
                        st["C32"], st["n32"], st["Cb"], st["nb"] = Cnew, nnew, Cbnew, nbnew
                        for h in range(4):
                            pn = pn0 if h < 2 else pn1
                            reg = pn[:, (h % 2) * 256:(h % 2 + 1) * 256]
                            mm(reg, sd[:, h * 128:(h + 1) * 128], v_[:, c, h * 256:(h + 1) * 256], True, False, [sd, v_], [pn])
                            mm(reg, q_[:, h * 2, cs_], Cbold[:, h, 0, :], False, False, [q_, Cbold], [pn])
                            mm(reg, q_[:, h * 2 + 1, cs_], Cbold[:, h, 1, :], False, True, [q_, Cbold], [pn])
                        for h in range(4):
                            mm(pD[:, h:h + 1], sd[:, h * 128:(h + 1) * 128], ones_bf[:, 0:1], True, False, [sd, ones_bf], [pD])
                            mm(pD[:, h:h + 1], q_[:, h * 2, cs_], nbold[:, h, 0:1], False, False, [q_, nbold], [pD])
                            mm(pD[:, h:h + 1], q_[:, h * 2 + 1, cs_], nbold[:, h, 1:2], False, True, [q_, nbold], [pD])
                        dn = sm.next()
                        dve(lambda e, dn=dn: e.tensor_tensor(out=dn[:, 8:12], in0=pD[:, 0:4], in1=EFL[s][:, cg, d, :], op=ALU.max),
                            [pD, EFL[s]], [dn])
                        dve(lambda e, dn=dn: e.scalar_tensor_tensor(out=dn[:, 0:4], in0=pD[:, 0:4], scalar=-1.0, in1=dn[:, 8:12],
                                                                    op0=ALU.mult, op1=ALU.max), [pD, dn], [dn])
                        dve(lambda e, dn=dn: e.reciprocal(out=dn[:, 4:8], in_=dn[:, 0:4]), [dn], [dn])
                        if not (last and s == 1):
                            hb = hbuf.next()
                            for h in range(4):
                                pn = pn0 if h < 2 else pn1
                                act(lambda e, hb=hb, pn=pn, h=h, dn=dn: e.activation(
                                    out=hb[:, h * 256:(h + 1) * 256], in_=pn[:, (h % 2) * 256:(h % 2 + 1) * 256], func=AF.Identity,
                                    scale=dn[:, 4 + h:5 + h]), [pn, dn], [hb])
                            dma("sp", hdst[tg:tg + 128, :], hb[:], [hb], [])

                    gen = chunk_gen()
                    cur = next(gen)
                    cur_p = prep(cur)
                    while cur is not None:
                        nxt = next(gen, None)
                        nxt_p = prep(nxt) if nxt is not None else None
                        main(cur, cur_p)
                        cur, cur_p = nxt, nxt_p
                    P.barrier()
                bdir(0)
                bdir(1)

            with ExitStack() as es:
                hfin = Ring([sb(es, "hfin%d" % i, [128, 1024]) for i in range(3)])
                hbin = Ring([sb(es, "hbin%d" % i, [128, 1024]) for i in range(3)])
                hn = Ring([sb(es, "hn%d" % i, [128, 1024]) for i in range(2)])
                gbb = Ring([sb(es, "gbb%d" % i, [128, 8, 512], BF16) for i in range(3)])
                ybb = Ring([sb(es, "ybb%d" % i, [128, 8, 512], BF16) for i in range(3)])
                tiles3 = []
                junk = sb(es, "junk", [128, 256])
                sm3 = Ring([sb(es, "sm3_%d" % i, [128, 16]) for i in range(4)])
                ptr = Ring(psb)
                for (s, t0, n) in all_blocks:
                    if last and s == 1:
                        continue
                    g0 = GOFF[s] + t0
                    for c in range(n // 128):
                        tiles3.append((g0 + c * 128, slice(c * 128, (c + 1) * 128), c == 0, c == n // 128 - 1, g0, n))
                blkbufs = {}

                def b3_stage_a(tl):
                    tg, _, is_first, _, g0, n = tl
                    if is_first:
                        gb_ = gbb.next(); yb_ = ybb.next()
                        dma("act", gb_[:, :, 0:n], gBT[:, g0:g0 + n].rearrange("(c p) t -> p c t", p=128), [], [gb_])
                        blkbufs[g0] = (gb_, yb_)
                    hfi = hfin.next(); hbi = hbin.next()
                    dma("sp", hfi[:], hf_d[tg:tg + 128, :], [], [hfi])
                    dma("sp", hbi[:], hbk_d[tg:tg + 128, :], [], [hbi])
                    pool(lambda e, hfi=hfi, hbi=hbi: e.tensor_tensor(out=hfi[:], in0=hfi[:], in1=hbi[:], op=ALU.add), [hfi, hbi], [hfi])
                    st_ = sm3.next()
                    for h in range(4):
                        act(lambda e, hfi=hfi, h=h, st_=st_: e.activation(
                            out=junk[:], in_=hfi[:, h * 256:(h + 1) * 256], func=AF.Square, accum_out=st_[:, h:h + 1]),
                            [hfi], [junk, st_])
                    act(lambda e, st_=st_: e.activation(out=st_[:, 0:4], in_=st_[:, 0:4], func=AF.Sqrt, bias=epsc[:], scale=1.0 / 256),
                        [st_, epsc], [st_])
                    dve(lambda e, st_=st_: e.reciprocal(out=st_[:, 4:8], in_=st_[:, 0:4]), [st_], [st_])
                    return hfi, st_

                def b3_stage_b(tl, sa):
                    tg, cs_, _, is_last, g0, n = tl
                    gb_, yb_ = blkbufs[g0]
                    hfi, st_ = sa
                    hn_ = hn.next()
                    for h in range(4):
                        act(lambda e, hn_=hn_, hfi=hfi, st_=st_, h=h: e.activation(
                            out=hn_[:, h * 256:(h + 1) * 256], in_=hfi[:, h * 256:(h + 1) * 256], func=AF.Identity,
                            scale=st_[:, 4 + h:5 + h]), [hfi, st_], [hn_])
                    for half in range(2):
                        pt = ptr.next()
                        for c4 in range(4):
                            chn = half * 4 + c4
                            pe(lambda e, pt=pt, hn_=hn_, chn=chn, c4=c4: e.transpose(
                                out=pt[:, c4 * 128:(c4 + 1) * 128], in_=hn_[:, chn * 128:(chn + 1) * 128], identity=ident[:]),
                                [hn_, ident], [pt])
                        for c4 in range(4):
                            chn = half * 4 + c4
                            dve(lambda e, pt=pt, chn=chn, c4=c4, yb_=yb_, gb_=gb_, cs_=cs_: e.scalar_tensor_tensor(
                                out=yb_[:, chn, cs_], in0=pt[:, c4 * 128:(c4 + 1) * 128], scalar=mnT[:, l, chn:chn + 1],
                                in1=gb_[:, chn, cs_], op0=ALU.mult, op1=ALU.mult), [pt, mnT, gb_], [yb_])
                    if is_last:
                        dma("sp", ymT[512:1536, g0:g0 + n].rearrange("(c p) t -> p c t", p=128), yb_[:, :, 0:n], [yb_], [])

                if tiles3:
                    sa_cur = b3_stage_a(tiles3[0])
                    for i3, tl in enumerate(tiles3):
                        sa_nxt = b3_stage_a(tiles3[i3 + 1]) if i3 + 1 < len(tiles3) else None
                        b3_stage_b(tl, sa_cur)
                        sa_cur = sa_nxt
                P.barrier()
            if stop_after == "B":
                return True

            with ExitStack() as co:
                ycT = sb(co, "ycT", [128, 4, TT], BF16)
                with ExitStack() as es:
                    CH = 32
                    m32f = sb(es, "m32f", [32, 32])
                    m32b = sb(es, "m32b", [32, 32])
                    identc_bf = sb(es, "identc_bf", [128, 128], BF16)
                    pool(lambda e: e.tensor_copy(out=identc_bf[:], in_=ident[:]), [ident], [identc_bf])
                    pool(lambda e: e.tensor_copy(out=m32f[:], in_=tri_f[0:32, 0:32]), [tri_f], [m32f])
                    pool(lambda e: e.tensor_copy(out=m32b[:], in_=tri_b[0:32, 0:32]), [tri_b], [m32b])
                    lfb = Ring([sb(es, "lfb%d" % i, [128, 4, 512]) for i in range(2)])
                    qqb = Ring([sb(es, "qqb%d" % i, [128, 4, 512], BF16) for i in range(2)])
                    kkb = Ring([sb(es, "kkb%d" % i, [128, 4, 512], BF16) for i in range(2)])
                    vvb = Ring([sb(es, "vvb%d" % i, [32, 8, 512], BF16) for i in range(4)])
                    Pc = sb(es, "Pc", [128, 4, 512])
                    Pr = sb(es, "Pr", [128, 4, 512])
                    e1 = sb(es, "e1", [128, 4, 512])
                    e2 = sb(es, "e2", [128, 4, 512])
                    PBs = [dict(qt=sb(es, "qt%d" % i, [128, 4, 512], BF16), ktl=sb(es, "ktl%d" % i, [128, 4, 512], BF16),
                                qh=sb(es, "qh%d" % i, [128, 4, 512], BF16), kh=sb(es, "kh%d" % i, [128, 4, 512], BF16),
                                ecl=sb(es, "ecl%d" % i, [128, 4, 16])) for i in range(2)]
                    smr = sb(es, "smr", [128, 4, 16])
                    smc = sb(es, "smc", [128, 4, 16])
                    smx = sb(es, "smx", [128, 4, 16])
                    oblk = Ring([sb(es, "oblk%d" % i, [128, 4, 512]) for i in range(2)])
                    ofin = e2
                    gcb = sb(es, "gcb", [128, 4, 512], BF16)
                    sqb = sb(es, "sqb", [128, 4, 512], BF16)
                    rsb = e1
                    scT = Ring([sb(es, "scT%d" % i, [32, 128], BF16) for i in range(4)])
                    ktok = Ring([sb(es, "ktok%d" % i, [32, 512], BF16) for i in range(4)])
                    S32r = Ring([sb(es, "S32_%d" % i, [128, 4, 128]) for i in range(2)])
                    Sbr = Ring([sb(es, "Sb_%d" % i, [128, 4, 128], BF16) for i in range(3)])
                    pSC2, pK2, pU2 = psb[0:2], psb[2:4], psb[4:6]
                    pO1, pR1 = psb[6], psb[7]
                    cst = [0, 0, 0]

                    def cdir(d):
                        cstate = {"S32": S32r.next(), "Sb": Sbr.next()}
                        dve(lambda e, t_=cstate["S32"]: e.memset(t_[:], 0.0), [], [cstate["S32"]])
                        pool(lambda e, t_=cstate["Sb"]: e.memset(t_[:], 0.0), [], [cstate["Sb"]])
                        m32 = m32f if d == 0 else m32b
                        blks = list(all_blocks)
                        if d == 1:
                            blks = [b_ for b_ in blks if b_[0] == 1][::-1] + [b_ for b_ in blks if b_[0] == 0][::-1]

                        def make_prep(bk, PB):
                            s, t0, n = bk
                            g0 = GOFF[s] + t0
                            nj = n // CH
                            qt, ktl, qh, kh, ecl = PB["qt"], PB["ktl"], PB["qh"], PB["kh"], PB["ecl"]
                            lf_ = lfb.next(); qq_ = qqb.next(); kk_ = kkb.next()
                            vvs = [vvb.next() for _ in range((nj + 7) // 8)]
                            PB["vv"] = vvs
                            th = []

                            def loads():
                                dma("sp", lf_[:, :, 0:n], lfT[d][:, g0:g0 + n].rearrange("(h p) t -> p h t", p=128), [], [lf_])
                                dma("act", qq_[:, :, 0:n], cqT[:, g0:g0 + n].rearrange("(h p) t -> p h t", p=128), [], [qq_])
                                dma("act", kk_[:, :, 0:n], ckT[d][:, g0:g0 + n].rearrange("(h p) t -> p h t", p=128), [], [kk_])
                                for hi_, vv_ in enumerate(vvs):
                                    nn = min(8, nj - hi_ * 8)
                                    dma("sp", vv_[:, 0:nn, :], ci_d[g0 + hi_ * 256:g0 + hi_ * 256 + nn * CH, :].rearrange("(j i) f -> i j f", i=CH),
                                        [], [vv_])
                            th.append((0, loads))

                            def scans():
                                for h in range(4):
                                    dve(lambda e, h=h: e.tensor_tensor_scan(
                                        out=Pc[:, h, 0:n], data0=fap(ones_f[:], [[0, n]]), data1=lf_[:, h, 0:n], initial=0.0, op0=ALU.mult, op1=ALU.add),
                                        [lf_, ones_f], [Pc])
                            th.append((1, scans))

                            def pool_chain():
                                pool(lambda e: e.tensor_copy(out=Pr[:, :, 0:CH], in_=Pc[:, :, 0:CH]), [Pc], [Pr])
                                if nj > 1:
                                    for h in range(4):
                                        pool(lambda e, h=h: e.tensor_tensor(
                                            out=fap(Pr[:], [[CH, nj - 1], [1, CH]], off=h * 512 + CH),
                                            in0=fap(Pc[:], [[CH, nj - 1], [1, CH]], off=h * 512 + CH),
                                            in1=fap(Pc[:], [[CH, nj - 1], [0, CH]], off=h * 512 + CH - 1), op=ALU.subtract), [Pc], [Pr])
                                pool(lambda e: e.tensor_copy(out=smc[:, :, 0:nj], in_=fap(Pr[:], [[512, 4], [CH, nj]], off=CH - 1)), [Pr], [smc])
                                if d == 1:
                                    pool(lambda e: e.tensor_tensor(out=Pr[:, :, 0:n], in0=Pr[:, :, 0:n], in1=lf_[:, :, 0:n], op=ALU.subtract),
                                         [Pr, lf_], [Pr])
                                mid = 15 if d == 0 else 16
                                pool(lambda e: e.tensor_copy(out=smr[:, :, 0:nj], in_=fap(Pr[:], [[512, 4], [CH, nj]], off=mid)), [Pr], [smr])
                                pool(lambda e: e.tensor_tensor(out=smx[:, :, 0:nj], in0=smc[:, :, 0:nj], in1=smr[:, :, 0:nj], op=ALU.subtract),
                                     [smc, smr], [smx])
                                for h in range(4):
                                    pool(lambda e, h=h: e.tensor_tensor(
                                        out=fap(Pc[:], [[CH, nj], [1, CH]], off=h * 512), in0=fap(Pr[:], [[CH, nj], [1, CH]], off=h * 512),
                                        in1=fap(smr[:], [[1, nj], [0, CH]], off=h * 16), op=ALU.subtract), [Pr, smr], [Pc])
                            th.append((2, pool_chain))

                            def exps():
                                act(lambda e: e.activation(out=e1[:, :, 0:n], in_=Pc[:, :, 0:n], func=AF.Exp), [Pc], [e1])
                                act(lambda e: e.activation(out=e2[:, :, 0:n], in_=Pc[:, :, 0:n], func=AF.Exp, scale=-1.0), [Pc], [e2])
                                act(lambda e: e.activation(out=ecl[:, :, 0:nj], in_=smc[:, :, 0:nj], func=AF.Exp), [smc], [ecl])
                                act(lambda e: e.activation(out=smr[:, :, 0:nj], in_=smr[:, :, 0:nj], func=AF.Exp), [smr], [smr])
                                act(lambda e: e.activation(out=smx[:, :, 0:nj], in_=smx[:, :, 0:nj], func=AF.Exp), [smx], [smx])
                            th.append((11, exps))
                            eq, ek_ = (e1, e2) if d == 0 else (e2, e1)
                            fq, fk = (smr, smx) if d == 0 else (smx, smr)

                            def pool_fin():
                                pool(lambda e: e.tensor_tensor(out=qt[:, :, 0:n], in0=qq_[:, :, 0:n], in1=eq[:, :, 0:n], op=ALU.mult), [qq_, eq], [qt])
                                pool(lambda e: e.tensor_tensor(out=ktl[:, :, 0:n], in0=kk_[:, :, 0:n], in1=ek_[:, :, 0:n], op=ALU.mult), [kk_, ek_], [ktl])
                                for h in range(4):
                                    pool(lambda e, h=h: e.tensor_tensor(
                                        out=fap(qh[:], [[CH, nj], [1, CH]], off=h * 512), in0=fap(qt[:], [[CH, nj], [1, CH]], off=h * 512),
                                        in1=fap(fq[:], [[1, nj], [0, CH]], off=h * 16), op=ALU.mult), [qt, fq], [qh])
                                    pool(lambda e, h=h: e.tensor_tensor(
                                        out=fap(kh[:], [[CH, nj], [1, CH]], off=h * 512), in0=fap(ktl[:], [[CH, nj], [1, CH]], off=h * 512),
                                        in1=fap(fk[:], [[1, nj], [0, CH]], off=h * 16), op=ALU.mult), [ktl, fk], [kh])
                            th.append((12, pool_fin))
                            return th

                        def make_readout(bk, ob):
                            s, t0, n = bk
                            g0 = GOFF[s] + t0
                            th = []
                            if d == 0:
                                if not (last and s == 1):
                                    th.append((0, lambda: dma("sp", ofT[:, g0:g0 + n].rearrange("(h p) t -> p h t", p=128), ob[:, :, 0:n], [ob], [])))
                                return th
                            if last and s == 1:
                                return th

                            def loads():
                                dma("sp", ofin[:, :, 0:n], ofT[:, g0:g0 + n].rearrange("(h p) t -> p h t", p=128), [], [ofin])
                                dma("act", gcb[:, :, 0:n], gCT[:, g0:g0 + n].rearrange("(h p) t -> p h t", p=128), [], [gcb])
                            th.append((0, loads))
                            th.append((1, lambda: pool(lambda e: e.tensor_tensor(out=ob[:, :, 0:n], in0=ob[:, :, 0:n], in1=ofin[:, :, 0:n], op=ALU.add),
                                                       [ob, ofin], [ob])))
                            th.append((2, lambda: act(lambda e: e.activation(out=sqb[:, :, 0:n], in_=ob[:, :, 0:n], func=AF.Square), [ob], [sqb])))
                            for tq in range(n // 128):
                                def grp(tq=tq):
                                    for h in range(4):
                                        mm(pR1[:, h * 128:(h + 1) * 128], ones_bf[:], sqb[:, h, tq * 128:(tq + 1) * 128], True, True, [ones_bf, sqb], [pR1])
                                    act(lambda e: e.activation(out=rsb[:, :, tq * 128:(tq + 1) * 128],
                                                               in_=pR1[:, 0:512].rearrange("p (h t) -> p h t", h=4), func=AF.Ln, bias=epsc[:],
                                                               scale=1.0 / 128), [pR1, epsc], [rsb])
                                th.append((3 + tq, grp))
                            th.append((7, lambda: act(lambda e: e.activation(out=rsb[:, :, 0:n], in_=rsb[:, :, 0:n], func=AF.Exp, scale=-0.5), [rsb], [rsb])))

                            def stts():
                                for h in range(4):
                                    dve(lambda e, h=h: e.scalar_tensor_tensor(
                                        out=rsb[:, h, 0:n], in0=ob[:, h, 0:n], scalar=hnT[:, l, h:h + 1], in1=rsb[:, h, 0:n],
                                        op0=ALU.mult, op1=ALU.mult), [ob, hnT, rsb], [rsb])
                            th.append((8, stts))

                            def gmul():
                                for h in range(4):
                                    pool(lambda e, h=h: e.tensor_tensor(out=rsb[:, h, 0:n], in0=rsb[:, h, 0:n], in1=gcb[:, h, 0:n], op=ALU.mult),
                                         [rsb, gcb], [rsb])
                            th.append((9, gmul))

                            def wr():
                                for h in range(4):
                                    if s == 1:
                                        o_ap = ycT[:, h, t0:t0 + n]
                                        i0 = rsb[:, h, 0:n]
                                    else:
                                        ncb_ = n // ROWS
                                        c0 = t0 // ROWS
                                        o_ap = ycT[:, h, TC:TT].rearrange("p (r c) -> p c r", c=64)[:, c0:c0 + ncb_, :]
                                        i0 = rsb[:, h, 0:n].rearrange("p (c r) -> p c r", r=ROWS)
                                    act(lambda e, o_ap=o_ap, i0=i0: e.copy(out=o_ap, in_=i0), [rsb], [ycT])
                            th.append((10, wr))
                            return th

                        def run_chunks(bk, PB, ob, side):
                            s, t0, n = bk
                            nj = n // CH
                            qt, ktl, qh, kh, ecl, vvs = PB["qt"], PB["ktl"], PB["qh"], PB["kh"], PB["ecl"], PB["vv"]
                            jlist = list(range(nj) if d == 0 else range(nj - 1, -1, -1))
                            side = sorted([(min(sl * nj // 16, nj - 1), k_, f_) for k_, (sl, f_) in enumerate(side)], key=lambda x: (x[0], x[1]))

                            def vv_of(j):
                                return vvs[j // 8], j % 8

                            def stage1(j):
                                js = slice(j * CH, (j + 1) * CH)
                                pSCj = pSC2[cst[0] % 2]
                                pKj = pK2[cst[0] % 2]
                                cst[0] += 1
                                for h in range(4):
                                    mm(pSCj[0:32, h * 32:(h + 1) * 32], ktl[:, h, js], qt[:, h, js], True, True, [ktl, qt], [pSCj])
                                sc_ = scT.next()
                                dve(lambda e, sc_=sc_, pSCj=pSCj: e.tensor_tensor(out=sc_[:].rearrange("p (h t) -> p h t", h=4),
                                                                        in0=pSCj[0:32, 0:128].rearrange("p (h t) -> p h t", h=4),
                                                                        in1=fap(m32[:], [[0, 4], [1, 32]]), op=ALU.mult), [pSCj, m32], [sc_])
                                pkv = pKj[:, :].bitcast(BF16)
                                for h in range(4):
                                    pe(lambda e, h=h, js=js, pkv=pkv: e.transpose(out=pkv[0:32, h * 128:(h + 1) * 128], in_=kh[:, h, js],
                                                                                     identity=identc_bf[:]), [kh, identc_bf], [pKj])
                                kt_ = ktok.next()
                                act(lambda e, kt_=kt_, pkv=pkv: e.copy(out=kt_[:], in_=pkv[0:32, 0:512]), [pKj], [kt_])
                                return sc_, kt_

                            def stage_u(j, kt_):
                                vv_, jj = vv_of(j)
                                pUj = pU2[cst[1] % 2]
                                cst[1] += 1
                                for h in range(4):
                                    mm(pUj[:, h * 128:(h + 1) * 128], kt_[0:32, h * 128:(h + 1) * 128], vv_[0:32, jj, h * 128:(h + 1) * 128], True, True,
                                       [kt_, vv_], [pUj])
                                Sold = cstate["S32"]
                                Snew, Sbnew = S32r.next(), Sbr.next()
                                dve(lambda e, j=j, Sold=Sold, Snew=Snew: e.tensor_tensor(
                                    out=Snew[:], in0=Sold[:], in1=fap(ecl[:], [[16, 4], [0, 128]], off=j), op=ALU.mult), [Sold, ecl], [Snew])
                                dve(lambda e, Snew=Snew, pUj=pUj: e.tensor_tensor(
                                    out=Snew[:].rearrange("p h e -> p (h e)"), in0=Snew[:].rearrange("p h e -> p (h e)"), in1=pUj[:, 0:512],
                                    op=ALU.add), [Snew, pUj], [Snew])
                                act(lambda e, Sbnew=Sbnew, Snew=Snew: e.copy(out=Sbnew[:], in_=Snew[:]), [Snew], [Sbnew])
                                sb_before = cstate["Sb"]
                                cstate["S32"], cstate["Sb"] = Snew, Sbnew
                                return sb_before

                            def stage_o(j, sc_, Sbold):
                                vv_, jj = vv_of(j)
                                js = slice(j * CH, (j + 1) * CH)
                                for h in range(4):
                                    reg = pO1[:, h * 32:(h + 1) * 32]
                                    mm(reg, vv_[0:32, jj, h * 128:(h + 1) * 128], sc_[0:32, h * 32:(h + 1) * 32], True, False, [vv_, sc_], [pO1])
                                    mm(reg, Sbold[:, h, :], qh[:, h, js], False, True, [Sbold, qh], [pO1])
                                act(lambda e, js=js: e.copy(out=ob[:, :, js], in_=pO1[:, 0:128].rearrange("p (h t) -> p h t", h=4)),
                                    [pO1], [ob])

                            nj_ = len(jlist)
                            s1 = {}
                            sbin = {}
                            for i in range(min(2, nj_)):
                                s1[i] = stage1(jlist[i])
                            sbin[0] = stage_u(jlist[0], s1[0][1])
                            si = 0
                            for i in range(nj_):
                                if i + 2 < nj_:
                                    s1[i + 2] = stage1(jlist[i + 2])
                                if i + 1 < nj_:
                                    sbin[i + 1] = stage_u(jlist[i + 1], s1[i + 1][1])
                                stage_o(jlist[i], s1[i][0], sbin[i])
                                while si < len(side) and side[si][0] <= i:
                                    side[si][2]()
                                    si += 1
                            while si < len(side):
                                side[si][2]()
                                si += 1

                        for _, t_ in make_prep(blks[0], PBs[0]):
                            t_()
                        pending = []
                        for bi, bk in enumerate(blks):
                            side = list(pending)
                            if bi + 1 < len(blks):
                                side += make_prep(blks[bi + 1], PBs[(bi + 1) % 2])
                            ob = oblk.next()
                            run_chunks(bk, PBs[bi % 2], ob, side)
                            pending = make_readout(bk, ob)
                        for _, t_ in pending:
                            t_()
                        P.barrier()
                    cdir(0)
                    cdir(1)
                if dbg:
                    dbg_yc = nc.dram_tensor("dbg_yc%d" % l, [128, 4, TT], BF16, kind="ExternalOutput").ap()
                    dma("sp", dbg_yc, ycT[:], [ycT], [])
                if stop_after == "C":
                    return True

                with ExitStack() as es:
                    wo = sb(es, "wo", [128, KD, D], BF16)
                    ymb = Ring([sb(es, "ymb%d" % i, [128, 12, 512], BF16) for i in range(2)])
                    hjb = Ring([sb(es, "hjb%d" % i, [128, 4, 512]) for i in range(2)])
                    hob = Ring([sb(es, "hob%d" % i, [128, 4, 512]) for i in range(2)])
                    pring = Ring(psb)
                    for c in range(8):
                        dma("pool", wo[:, :, c * 256:(c + 1) * 256], w_out[l, :, c * 256:(c + 1) * 256].rearrange("(k p) f -> p k f", p=128), [], [wo])
                    for (s, t0, n) in all_blocks:
                        if last and s == 1:
                            continue
                        g0 = GOFF[s] + t0
                        ym = ymb.next()
                        dma("sp", ym[:, :, 0:n], ymT[:, g0:g0 + n].rearrange("(k p) t -> p k t", p=128), [], [ym])
                        for jg in range(4):
                            hj = hjb.next(); ho = hob.next()
                            dma("act", hj[:, :, 0:n], src_h[s][jg * 512:(jg + 1) * 512, t0:t0 + n].rearrange("(j p) t -> p j t", p=128), [], [hj])
                            for jj in range(4):
                                j = jg * 4 + jj
                                pb = pring.next()
                                for k in range(KD):
                                    rhs = ym[:, k, 0:n] if k < 12 else ycT[:, k - 12, g0:g0 + n]
                                    mm(pb[:, 0:n], wo[:, k, j * 128:(j + 1) * 128], rhs, k == 0, k == KD - 1, [wo, ym, ycT], [pb])
                                dve(lambda e, pb=pb, j=j, jj=jj, hj=hj, ho=ho, n=n, s=s: e.scalar_tensor_tensor(
                                    out=ho[:, jj, 0:n], in0=pb[:, 0:n], scalar=modT[:, l, s, 32 + j:33 + j], in1=hj[:, jj, 0:n],
                                    op0=ALU.mult, op1=ALU.add), [pb, modT, hj], [ho])
                            dma("sp", hT[s][jg * 512:(jg + 1) * 512, t0:t0 + n].rearrange("(j p) t -> p j t", p=128), ho[:, :, 0:n], [ho], [])
                    P.barrier()
            return False

        def final_norm():
            with ExitStack() as es:
                hin = [sb(es, "fhin%d" % i, [128, KD, 512]) for i in range(2)]
                sq = sb(es, "fsq", [128, KD, 512], BF16)
                rstd = sb(es, "frstd", [128, 512])
                ot = [sb(es, "fot%d" % i, [128, KD, 512]) for i in range(2)]
                for bi, (t0, n) in enumerate(blocks(0)):
                    h = hin[bi % 2]
                    o = ot[bi % 2]
                    dma("sp", h[:, :, 0:n], hT[0][:, t0:t0 + n].rearrange("(k p) t -> p k t", p=128), [], [h])
                    act(lambda e, h=h, n=n: e.activation(out=sq[:, :, 0:n], in_=h[:, :, 0:n], func=AF.Square), [h], [sq])
                    pss = psb[bi % 2]
                    for k in range(KD):
                        mm(pss[:, 0:n], ones_bf[:], sq[:, k, 0:n], k == 0, k == KD - 1, [ones_bf, sq], [pss])
                    act(lambda e, pss=pss, n=n: e.activation(out=rstd[:, 0:n], in_=pss[:, 0:n], func=AF.Sqrt,
                                                             bias=epsc[:], scale=1.0 / D), [pss, epsc], [rstd])
                    dve(lambda e, n=n: e.reciprocal(out=rstd[:, 0:n], in_=rstd[:, 0:n]), [rstd], [rstd])
                    for k in range(KD):
                        dve(lambda e, h=h, o=o, k=k, n=n: e.scalar_tensor_tensor(
                            out=o[:, k, 0:n], in0=h[:, k, 0:n], scalar=fnT[:, k:k + 1], in1=rstd[:, 0:n],
                            op0=ALU.mult, op1=ALU.mult), [h, fnT, rstd], [o])
                    dma("act", outT[:, t0:t0 + n].rearrange("(k p) t -> p k t", p=128), o[:, :, 0:n], [o], [])
                P.barrier()

        stopped = False
        for l in range(DEPTH):
            if layer(l):
                stopped = True
                break
        if not stopped:
            final_norm()
        P.barrier()
        P.emit()
    return nc


def host_layout(inp, b, DEPTH):
    f = lambda a: np.ascontiguousarray(a, dtype=np.float32)
    d = {}
    d["xT"] = f(inp["x"][b].T)
    d["cxT"] = f(inp["ctx"][b].T)
    cs = np.stack([inp["c"][b].reshape(KD, 128).T, inp["c_ctx"].reshape(KD, 128).T], axis=-1)
    d["cs"] = f(cs)
    d["w_ada"] = f(inp["w_ada"])
    d["badaT"] = f(inp["b_ada"].reshape(DEPTH, 48, 128).transpose(2, 0, 1))
    d["normgT"] = f(inp["norm_g"].reshape(DEPTH, KD, 128).transpose(2, 0, 1))
    d["w_in"] = f(inp["w_in"])
    d["b_in"] = f(inp["b_in"])
    idx = np.array(FM_STARTS)[None, :] + np.arange(128)[:, None]
    d["bfm"] = f(inp["b_in"][:, idx].transpose(1, 0, 2))
    d["wsT"] = f(inp["w_spatial"].transpose(0, 3, 1, 2))
    d["bs"] = f(inp["b_spatial"].reshape(DEPTH, 512))
    d["convT"] = f(inp["conv_qk"].reshape(DEPTH, 3, 16, 128).transpose(3, 0, 2, 1))
    d["mnT"] = f(inp["mlstm_norm"].reshape(DEPTH, 8, 128).transpose(2, 0, 1))
    d["hnT"] = f(inp["hgrn_norm"].reshape(DEPTH, 4, 128).transpose(2, 0, 1))
    d["lbT"] = f(inp["hgrn_lb_logits"].reshape(DEPTH, 4, 128).transpose(2, 1, 0))
    d["fnT"] = f(inp["final_norm"].reshape(KD, 128).T)
    d["w_out"] = f(inp["w_out"])
    return d


_NC_CACHE = {}


def kernel(**inputs):
    inp = {k: np.asarray(v) for k, v in inputs.items()}
    B, TL, _ = inp["x"].shape
    TC = inp["ctx"].shape[1]
    DEPTH = inp["w_in"].shape[0]
    key = (TL, TC, DEPTH)
    if key not in _NC_CACHE:
        _NC_CACHE[key] = build(TL, TC, DEPTH)
    nc = _NC_CACHE[key]
    in_maps = [host_layout(inp, b, DEPTH) for b in range(B)]
    res = run_bass_kernel_spmd(nc, in_maps, core_ids=list(range(B)))
    out = np.stack([np.ascontiguousarray(r["outT"].T) for r in res.results], axis=0)
    return out.astype(np.float32)
```
